# Optimizing a Trainium2 kernel written in Bass

```python
import jax, jax.numpy as jnp
from jax import lax
import numpy as np

D_MODEL = 1024
BATCH = 8
SEQ = 4096
DEPTH = 4

LRU_WIDTH = D_MODEL
LRU_HEADS = 8
LRU_BLOCK = LRU_WIDTH // LRU_HEADS
CONV_WIDTH = 4
CONV_LEFT = 2
LRU_C = 8.0
SG_WIDTH = D_MODEL
SG_GROUPS = 8
SG_GROUP_DIM = SG_WIDTH // SG_GROUPS
SG_CHUNK = 128
EVEN_IN = 2 * LRU_WIDTH + 2 * SG_WIDTH
EVEN_MIX = LRU_WIDTH + SG_WIDTH
N_Q_HEADS = 16
N_KV_HEADS = 4
Q_PER_KV = N_Q_HEADS // N_KV_HEADS
HEAD_DIM = 64
WINDOW = 128
ATTN_BLOCK = 128
ATTN_SPAN = ATTN_BLOCK + 2 * WINDOW
QKV_WIDTH = (N_Q_HEADS + 2 * N_KV_HEADS) * HEAD_DIM
ROPE_THETA = 10000.0
D_FF = -(-8 * D_MODEL // (3 * 256)) * 256
N_EVEN = (DEPTH + 1) // 2
N_ODD = DEPTH // 2
EPS = 1e-6
NEG_INF = -1e30

kernel_name = "hybrid_rglru_gmlp_swa_encoder"


def rmsnorm(x, g):
    xf = x.astype(jnp.float32)
    y = xf * lax.rsqrt(jnp.mean(xf * xf, axis=-1, keepdims=True) + EPS)
    return (y * g.astype(jnp.float32)).astype(x.dtype)


def layernorm(x, g, b):
    xf = x.astype(jnp.float32)
    mu = jnp.mean(xf, axis=-1, keepdims=True)
    var = jnp.mean(jnp.square(xf - mu), axis=-1, keepdims=True)
    y = (xf - mu) * lax.rsqrt(var + EPS)
    return (y * g.astype(jnp.float32) + b.astype(jnp.float32)).astype(x.dtype)


def rope(x, positions):
    half = x.shape[-1] // 2
    freqs = ROPE_THETA ** (-jnp.arange(half, dtype=jnp.float32) * 2.0 / x.shape[-1])
    ang = positions.astype(jnp.float32)[:, None] * freqs[None, :]
    cos = jnp.cos(ang)[None, :, None, :]
    sin = jnp.sin(ang)[None, :, None, :]
    xf = x.astype(jnp.float32)
    x1, x2 = xf[..., :half], xf[..., half:]
    out = jnp.concatenate([x1 * cos - x2 * sin, x2 * cos + x1 * sin], axis=-1)
    return out.astype(x.dtype)


def centred_depthwise_conv(x, w, b):
    S = x.shape[1]
    xp = jnp.pad(x, ((0, 0), (CONV_LEFT, CONV_WIDTH - 1 - CONV_LEFT), (0, 0)))
    y = b
    for k in range(CONV_WIDTH):
        y = y + xp[:, k:k + S, :] * w[k]
    return y


def rglru_scan(x, w_r, b_r, w_i, b_i, lam, reverse):
    B, S, W = x.shape
    xh = x.reshape(B, S, LRU_HEADS, LRU_BLOCK)
    f32 = jnp.float32
    r = jax.nn.sigmoid(jnp.einsum('bshi,hij->bshj', xh, w_r.astype(f32)).reshape(B, S, W) + b_r.astype(f32))
    i = jax.nn.sigmoid(jnp.einsum('bshi,hij->bshj', xh, w_i.astype(f32)).reshape(B, S, W) + b_i.astype(f32))
    log_a = -LRU_C * r * jax.nn.softplus(-lam.astype(f32))
    a = jnp.exp(log_a)
    mult = jnp.sqrt(jnp.maximum(-jnp.expm1(2.0 * log_a), 0.0))
    u = mult * (i * x)

    def combine(c1, c2):
        a1, b1 = c1
        a2, b2 = c2
        return a1 * a2, a2 * b1 + b2

    _, h = lax.associative_scan(combine, (a, u), reverse=reverse, axis=1)
    return h


def even_mixer(h, w_in, conv_w, conv_b, w_r, b_r, w_i, b_i, lam, ln_g, ln_b, sg_w, sg_b, w_out):
    B, S, _ = h.shape
    proj = h @ w_in
    xa, ga, zu, zv = jnp.split(proj, [LRU_WIDTH, 2 * LRU_WIDTH, 2 * LRU_WIDTH + SG_WIDTH], axis=-1)
    xa = centred_depthwise_conv(xa, conv_w, conv_b).astype(jnp.float32)
    h_fwd = rglru_scan(xa, w_r[0], b_r[0], w_i[0], b_i[0], lam[0], reverse=False)
    h_bwd = rglru_scan(xa, w_r[1], b_r[1], w_i[1], b_i[1], lam[1], reverse=True)
    y_a = jax.nn.gelu(ga) * (h_fwd + h_bwd).astype(h.dtype)
    u = jax.nn.gelu(zu)
    v = layernorm(jax.nn.gelu(zv), ln_g, ln_b)
    n_chunks = S // SG_CHUNK
    vc = v.reshape(B, n_chunks, SG_CHUNK, SG_GROUPS, SG_GROUP_DIM)
    sv = jnp.einsum('gpq,bcqgd->bcpgd', sg_w, vc) + jnp.transpose(sg_b)[None, None, :, :, None]
    y_b = u * sv.reshape(B, S, SG_WIDTH)
    return jnp.concatenate([y_a, y_b], axis=-1) @ w_out


def windowed_gqa(h, w_qkv, sinks, w_o):
    B, S, _ = h.shape
    positions = jnp.arange(S)
    qkv = h @ w_qkv
    q, k, v = jnp.split(qkv, [N_Q_HEADS * HEAD_DIM, (N_Q_HEADS + N_KV_HEADS) * HEAD_DIM], axis=-1)
    q = rope(q.reshape(B, S, N_Q_HEADS, HEAD_DIM), positions) * (HEAD_DIM ** -0.5)
    k = rope(k.reshape(B, S, N_KV_HEADS, HEAD_DIM), positions)
    v = v.reshape(B, S, N_KV_HEADS, HEAD_DIM)
    n_blocks = S // ATTN_BLOCK
    kp = jnp.pad(k, ((0, 0), (WINDOW, WINDOW), (0, 0), (0, 0)))
    vp = jnp.pad(v, ((0, 0), (WINDOW, WINDOW), (0, 0), (0, 0)))
    qb = q.reshape(B, n_blocks, ATTN_BLOCK, N_KV_HEADS, Q_PER_KV, HEAD_DIM).transpose(1, 0, 2, 3, 4, 5)
    sink = sinks.astype(jnp.float32).reshape(N_KV_HEADS, Q_PER_KV)

    def block(args):
        idx, qi = args
        start = idx * ATTN_BLOCK
        ki = lax.dynamic_slice_in_dim(kp, start, ATTN_SPAN, axis=1)
        vi = lax.dynamic_slice_in_dim(vp, start, ATTN_SPAN, axis=1)
        s = jnp.einsum('btkgd,bskd->bkgts', qi, ki).astype(jnp.float32)
        qpos = start + jnp.arange(ATTN_BLOCK)
        kpos = start - WINDOW + jnp.arange(ATTN_SPAN)
        valid = (jnp.abs(qpos[:, None] - kpos[None, :]) <= WINDOW) & (kpos >= 0)[None, :] & (kpos < S)[None, :]
        s = jnp.where(valid, s, NEG_INF)
        sink_col = jnp.broadcast_to(sink[None, :, :, None, None], s.shape[:-1] + (1,))
        p = jax.nn.softmax(jnp.concatenate([s, sink_col], axis=-1), axis=-1)[..., :ATTN_SPAN]
        return jnp.einsum('bkgts,bskd->btkgd', p.astype(vi.dtype), vi)

    out = lax.map(block, (jnp.arange(n_blocks), qb))
    out = out.transpose(1, 0, 2, 3, 4, 5).reshape(B, S, N_Q_HEADS * HEAD_DIM)
    return out @ w_o


def swiglu(h, w_gu, w_down):
    g, u = jnp.split(h @ w_gu, 2, axis=-1)
    return (jax.nn.silu(g) * u) @ w_down


def setup_inputs(seed: int = 0) -> dict:
    key = jax.random.key(seed)
    ks = jax.random.split(key, 24)
    nrm = jax.random.normal
    f32 = jnp.float32
    u = jax.random.uniform(ks[11], (N_EVEN, 2, LRU_WIDTH), f32, 0.9, 0.999)
    a0 = u ** (1.0 / LRU_C)
    return {
        "x": nrm(ks[0], (BATCH, SEQ, D_MODEL), f32),
        "mix_norm": 1.0 + 0.01 * nrm(ks[1], (DEPTH, D_MODEL), f32),
        "ffn_norm": 1.0 + 0.01 * nrm(ks[2], (DEPTH, D_MODEL), f32),
        "final_norm": 1.0 + 0.01 * nrm(ks[3], (D_MODEL,), f32),
        "even_w_in": nrm(ks[4], (N_EVEN, D_MODEL, EVEN_IN), f32) * D_MODEL ** -0.5,
        "even_conv_w": nrm(ks[5], (N_EVEN, CONV_WIDTH, LRU_WIDTH), f32) * CONV_WIDTH ** -0.5,
        "even_conv_b": 0.01 * nrm(ks[6], (N_EVEN, LRU_WIDTH), f32),
        "lru_w_r": nrm(ks[7], (N_EVEN, 2, LRU_HEADS, LRU_BLOCK, LRU_BLOCK), f32) * LRU_BLOCK ** -0.5,
        "lru_b_r": 0.01 * nrm(ks[8], (N_EVEN, 2, LRU_WIDTH), f32),
        "lru_w_i": nrm(ks[9], (N_EVEN, 2, LRU_HEADS, LRU_BLOCK, LRU_BLOCK), f32) * LRU_BLOCK ** -0.5,
        "lru_b_i": 0.01 * nrm(ks[10], (N_EVEN, 2, LRU_WIDTH), f32),
        "lru_lambda": jnp.log(a0) - jnp.log1p(-a0),
        "sg_ln_g": 1.0 + 0.01 * nrm(ks[12], (N_EVEN, SG_WIDTH), f32),
        "sg_ln_b": 0.01 * nrm(ks[13], (N_EVEN, SG_WIDTH), f32),
        "sg_w": nrm(ks[14], (N_EVEN, SG_GROUPS, SG_CHUNK, SG_CHUNK), f32) * SG_CHUNK ** -0.5,
        "sg_b": 1.0 + 0.01 * nrm(ks[15], (N_EVEN, SG_GROUPS, SG_CHUNK), f32),
        "even_w_out": nrm(ks[16], (N_EVEN, EVEN_MIX, D_MODEL), f32) * EVEN_MIX ** -0.5,
        "attn_w_qkv": nrm(ks[17], (N_ODD, D_MODEL, QKV_WIDTH), f32) * D_MODEL ** -0.5,
        "attn_sinks": 0.5 * nrm(ks[18], (N_ODD, N_Q_HEADS), f32),
        "attn_w_o": nrm(ks[19], (N_ODD, N_Q_HEADS * HEAD_DIM, D_MODEL), f32) * (N_Q_HEADS * HEAD_DIM) ** -0.5,
        "ffn_w_gu": nrm(ks[20], (DEPTH, D_MODEL, 2 * D_FF), f32) * D_MODEL ** -0.5,
        "ffn_w_down": nrm(ks[21], (DEPTH, D_FF, D_MODEL), f32) * D_FF ** -0.5,
    }


def reference(x, mix_norm, ffn_norm, final_norm, even_w_in, even_conv_w, even_conv_b,
              lru_w_r, lru_b_r, lru_w_i, lru_b_i, lru_lambda, sg_ln_g, sg_ln_b, sg_w, sg_b,
              even_w_out, attn_w_qkv, attn_sinks, attn_w_o, ffn_w_gu, ffn_w_down):
    for layer in range(DEPTH):
        j = layer // 2
        hn = rmsnorm(x, mix_norm[layer])
        if layer % 2 == 0:
            x = x + even_mixer(hn, even_w_in[j], even_conv_w[j], even_conv_b[j],
                               lru_w_r[j], lru_b_r[j], lru_w_i[j], lru_b_i[j], lru_lambda[j],
                               sg_ln_g[j], sg_ln_b[j], sg_w[j], sg_b[j], even_w_out[j])
        else:
            x = x + windowed_gqa(hn, attn_w_qkv[j], attn_sinks[j], attn_w_o[j])
        x = x + swiglu(rmsnorm(x, ffn_norm[layer]), ffn_w_gu[layer], ffn_w_down[layer])
    return rmsnorm(x, final_norm)
```

```python
import math
import numpy as np
from contextlib import ExitStack
import concourse.bass as bass
import concourse.mybir as mybir
from concourse.bass_utils import run_bass_kernel_spmd

F32 = mybir.dt.float32
BF16 = mybir.dt.bfloat16
AF = mybir.ActivationFunctionType
ALU = mybir.AluOpType
ENGS = ("pe", "act", "dve", "pool", "sp")

D = 1024
S = 4096
TT = 512
NT = S // TT
KC = D // 128
DFF = 2816
FC = DFF // 128
DEPTH = 4
EPS = 1e-6
NCORES = 8


class Buf:
    __slots__ = ("name", "w", "rd", "rdma", "slot", "semval", "last_dma")

    def __init__(self, name):
        self.name = name
        self.w = None
        self.rd = {}
        self.rdma = []
        self.slot = None
        self.semval = 0
        self.last_dma = None


class Slot:
    __slots__ = ("val", "sem", "idx")

    def __init__(self, idx):
        self.val = 0
        self.sem = None
        self.idx = idx


class Op:
    __slots__ = ("eng", "fn", "deps", "signal", "sem", "sigval", "is_dma", "ndma", "owner")

    def __init__(self, eng, fn, is_dma):
        self.eng = eng
        self.fn = fn
        self.deps = []
        self.signal = False
        self.sem = None
        self.sigval = 0
        self.is_dma = is_dma
        self.ndma = 0
        self.owner = None


class Prog:
    def __init__(self, nc):
        self.nc = nc
        self.ops = {e: [] for e in ENGS}
        self.slots = []
        self.free_slots = []
        self.active = []
        self.pending_dma = []
        self.need_bar = {e: None for e in ENGS}
        self.nops = 0

    def _dep(self, op, d, kind):
        if d is op:
            return
        if (not d.is_dma) and (not op.is_dma) and d.eng == op.eng:
            if op.eng == "pe":
                return
            if kind == "war":
                return
        d.signal = True
        op.deps.append(d)

    def add(self, eng, fn, reads=(), writes=(), owner=None, ndma=0, after=(), nobar=False):
        op = Op(eng, fn, owner is not None)
        if self.need_bar[eng] is not None:
            self._dep(op, self.need_bar[eng], "bar")
            self.need_bar[eng] = None
        for d in after:
            self._dep(op, d, "bar")
        for b in reads:
            if b.w is not None:
                self._dep(op, b.w, "raw")
        for b in writes:
            if b.w is not None:
                self._dep(op, b.w, "waw")
            for r in b.rd.values():
                self._dep(op, r, "war")
            for r in b.rdma:
                self._dep(op, r, "war")
        if owner is not None:
            op.owner = owner
            op.ndma = ndma
            if owner.last_dma is not None:
                self._dep(op, owner.last_dma, "chain")
            owner.last_dma = op
            if owner.slot is None:
                if (not nobar) and self.free_slots:
                    owner.slot = self.free_slots.pop()
                else:
                    owner.slot = Slot(len(self.slots))
                    self.slots.append(owner.slot)
                if not nobar:
                    self.active.append(owner)
            owner.slot.val += 16 * ndma
            op.sigval = owner.slot.val
            op.sem = owner.slot
            if not nobar:
                self.pending_dma.append(op)
        for b in reads:
            if op.is_dma:
                b.rdma.append(op)
            else:
                b.rd[eng] = op
        for b in writes:
            b.w = op
            b.rd = {}
            b.rdma = []
        self.ops[eng].append(op)
        self.nops += 1
        return op

    def barrier(self, dummies):
        da, dv, dp = dummies
        m_act = self.add("act", lambda e: e.activation(out=da[0:1, 0:1], in_=da[0:1, 1:2], func=AF.Copy))
        m_dve = self.add("dve", lambda e: e.memset(dv[0:1, 0:1], 0.0))
        hub = self.add("pool", lambda e: e.memset(dp[0:1, 0:1], 0.0), after=[m_act, m_dve] + self.pending_dma)
        self.pending_dma = []
        for o in self.active:
            self.free_slots.append(o.slot)
            o.slot = None
        self.active = []
        for e in ENGS:
            self.need_bar[e] = hub
        self.need_bar["pool"] = None
        return hub

    def emit(self, stack):
        nc = self.nc
        engsem = {e: stack.enter_context(nc.semaphore("cnt_" + e)) for e in ENGS}
        for sl in self.slots:
            sl.sem = stack.enter_context(nc.semaphore("dsl%d" % sl.idx))
        for e in ENGS:
            c = 0
            for op in self.ops[e]:
                if op.is_dma:
                    op.sem = op.sem.sem
                else:
                    op.sem = engsem[e]
                    if op.signal:
                        c += 1
                        op.sigval = c
        block = stack.enter_context(nc.Block())
        ops = self.ops

        def run(e, eh):
            seen = {}
            for op in ops[e]:
                waits = {}
                for d in op.deps:
                    k = id(d.sem)
                    if seen.get(k, 0) < d.sigval:
                        if k not in waits or waits[k][1] < d.sigval:
                            waits[k] = (d.sem, d.sigval)
                for k, (sem, val) in waits.items():
                    eh.wait_ge(sem, val)
                    seen[k] = val
                ins = op.fn(eh)
                if op.is_dma:
                    assert len(ins) == op.ndma, (len(ins), op.ndma)
                    for i in ins:
                        i.then_inc(op.sem, 16)
                elif op.signal:
                    ins.then_inc(op.sem, 1)

        @block.tensor
        def _(eh):
            run("pe", eh)

        @block.scalar
        def _(eh):
            run("act", eh)

        @block.vector
        def _(eh):
            run("dve", eh)

        @block.gpsimd
        def _(eh):
            run("pool", eh)

        @block.sync
        def _(eh):
            run("sp", eh)


class T:
    __slots__ = ("t", "b")

    def __init__(self, t, name):
        self.t = t
        self.b = Buf(name)


class SBAlloc:
    def __init__(self, nc, base, limit):
        self.nc = nc
        self.off = base
        self.limit = limit
        self.n = 0

    def tile(self, name, shape, dtype):
        esz = 4 if dtype == F32 else 2
        nbytes = esz
        for s in shape[1:]:
            nbytes *= s
        nbytes = (nbytes + 31) // 32 * 32
        assert self.off + nbytes <= self.limit, ("SBUF overflow", name, self.off, nbytes)
        self.n += 1
        t = self.nc.alloc_sbuf_tensor_at("%s_%d" % (name, self.n), list(shape), dtype, offset=self.off)
        self.off += nbytes
        return T(t, name)


def op_dma(P, eng, out_ap, in_ap, reads, writes, owner, **kw):
    return P.add(eng, lambda e: [e.dma_start(out=out_ap, in_=in_ap)], reads=reads, writes=writes, owner=owner, ndma=1, **kw)


def op_mm(P, ps_ap, lhsT, rhs, start, stop, reads, writes):
    return P.add("pe", lambda e: e.matmul(ps_ap, lhsT=lhsT, rhs=rhs, start=start, stop=stop), reads=reads, writes=writes)


def op_act(P, out, in_, func, reads, writes, **kw):
    return P.add("act", lambda e: e.activation(out=out, in_=in_, func=func, **kw), reads=reads, writes=writes)


def op_tt(P, eng, out, in0, in1, op, reads, writes):
    return P.add(eng, lambda e: e.tensor_tensor(out=out, in0=in0, in1=in1, op=op), reads=reads, writes=writes)


def op_ts(P, eng, out, in0, s1, s2, op0, op1, reads, writes):
    if op1 is None:
        return P.add(eng, lambda e: e.tensor_scalar(out=out, in0=in0, scalar1=s1, scalar2=None, op0=op0), reads=reads, writes=writes)
    return P.add(eng, lambda e: e.tensor_scalar(out=out, in0=in0, scalar1=s1, scalar2=s2, op0=op0, op1=op1), reads=reads, writes=writes)


def op_stt(P, out, in0, scalar, in1, op0, op1, reads, writes):
    return P.add("dve", lambda e: e.scalar_tensor_tensor(out=out, in0=in0, scalar=scalar, in1=in1, op0=op0, op1=op1), reads=reads, writes=writes)


def op_copy(P, eng, out, in_, reads, writes):
    return P.add(eng, lambda e: e.tensor_copy(out=out, in_=in_), reads=reads, writes=writes)


def op_memset(P, eng, out, val, writes):
    return P.add(eng, lambda e: e.memset(out, val), writes=writes)


PV_NORM = 0
PV_EVEN = 72
PV_EVEN_SZ = 88
PV_COLS = PV_EVEN + 2 * PV_EVEN_SZ

WSPEC_EVEN = [("win_fm", 24 * 128 * 1024), ("win_v", 1024 * 1024), ("gates", 8 * 128 * 512), ("sgwT", 128 * 1024),
              ("wout", 8 * 128 * 2048)]
WSPEC_ODD = [("wqk", 20 * 128 * 1024), ("wv", 1024 * 256), ("wo", 8 * 128 * 1024)]
WSPEC_FFN = [("wgu", FC * 128 * 2048), ("wdn", 8 * 128 * DFF)]


def fm(v):
    return np.ascontiguousarray(v.reshape(-1, 128).T)


def lhsT_arr(w, nchunk=None):
    K, N = w.shape
    a = w.reshape(K // 128, 128, N // 128, 128).transpose(2, 1, 0, 3)
    return np.ascontiguousarray(a)


def host_prep(inp):
    shared = {}
    pv = np.zeros((128, PV_COLS), np.float32)
    for l in range(DEPTH):
        pv[:, PV_NORM + 16 * l: PV_NORM + 16 * l + 8] = fm(inp["mix_norm"][l])
        pv[:, PV_NORM + 16 * l + 8: PV_NORM + 16 * l + 16] = fm(inp["ffn_norm"][l])
    pv[:, 64:72] = fm(inp["final_norm"])
    for j in range(2):
        b = PV_EVEN + j * PV_EVEN_SZ
        cw = inp["even_conv_w"][j]
        for k in range(4):
            pv[:, b + k: b + 32: 4] = fm(cw[k])
        pv[:, b + 32: b + 40] = fm(inp["even_conv_b"][j])
        pv[:, b + 40: b + 48] = fm(inp["lru_b_r"][j, 0])
        pv[:, b + 48: b + 56] = fm(inp["lru_b_i"][j, 0])
        pv[:, b + 56: b + 64] = fm(inp["lru_b_r"][j, 1])
        pv[:, b + 64: b + 72] = fm(inp["lru_b_i"][j, 1])
        pv[:, b + 72: b + 80] = fm(inp["lru_lambda"][j, 0])
        pv[:, b + 80: b + 88] = fm(inp["lru_lambda"][j, 1])
    shared["pvec"] = pv
    for j in range(2):
        w_in = inp["even_w_in"][j]
        shared["win_fm%d" % j] = lhsT_arr(w_in[:, :3072]).reshape(-1)
        shared["win_v%d" % j] = np.ascontiguousarray(w_in[:, 3072:]).reshape(-1)
        g = np.stack([inp["lru_w_r"][j, 0], inp["lru_w_i"][j, 0], inp["lru_w_r"][j, 1], inp["lru_w_i"][j, 1]], axis=2)
        shared["gates%d" % j] = np.ascontiguousarray(g).reshape(-1)
        shared["sgwT%d" % j] = np.ascontiguousarray(inp["sg_w"][j].transpose(2, 0, 1)).reshape(-1)
        shared["wout%d" % j] = lhsT_arr(inp["even_w_out"][j]).reshape(-1)
        shared["lng%d" % j] = np.ascontiguousarray(inp["sg_ln_g"][j].reshape(1, 1024))
        shared["lnb%d" % j] = np.ascontiguousarray(inp["sg_ln_b"][j].reshape(1, 1024))
        shared["sgb%d" % j] = np.ascontiguousarray(inp["sg_b"][j].reshape(1, 1024))
        wq = inp["attn_w_qkv"][j]
        q = wq[:, :1024]
        k = wq[:, 1024:1280]

        def swap_halves(m):
            K, N = m.shape
            return np.ascontiguousarray(m.reshape(K, N // 64, 2, 32)[:, :, ::-1, :].reshape(K, N))
        qk = np.concatenate([lhsT_arr(q), lhsT_arr(swap_halves(q)), lhsT_arr(k), lhsT_arr(swap_halves(k))], axis=0)
        shared["wqk%d" % j] = qk.reshape(-1)
        shared["wv%d" % j] = np.ascontiguousarray(wq[:, 1280:]).reshape(-1)
        shared["wo%d" % j] = lhsT_arr(inp["attn_w_o"][j]).reshape(-1)
        shared["sink%d" % j] = np.ascontiguousarray(inp["attn_sinks"][j].reshape(1, 16))
    for l in range(DEPTH):
        gu = inp["ffn_w_gu"][l]
        ga = lhsT_arr(gu[:, :DFF])
        ua = lhsT_arr(gu[:, DFF:])
        shared["wgu%d" % l] = np.ascontiguousarray(np.stack([ga, ua], axis=2)).reshape(-1)
        shared["wdn%d" % l] = lhsT_arr(inp["ffn_w_down"][l]).reshape(-1)
    half = 32
    freqs = (10000.0 ** (-np.arange(half, dtype=np.float32) * 2.0 / 64)).astype(np.float32)
    ang = np.arange(S, dtype=np.float32)[None, :] * freqs[:, None]
    cos = np.cos(ang).astype(np.float32)
    sin = np.sin(ang).astype(np.float32)
    cos64 = np.concatenate([cos, cos], axis=0)
    sin64 = np.concatenate([-sin, sin], axis=0)
    shared["cosT"] = np.ascontiguousarray(np.concatenate([cos64, cos64], axis=0))
    shared["sinT"] = np.ascontiguousarray(np.concatenate([sin64, sin64], axis=0))
    tk = np.arange(128)[:, None]
    tq = np.arange(128)[None, :]
    shared["masks"] = np.ascontiguousarray(np.stack([(tk >= tq), (tk <= tq)], axis=1).astype(np.float32))
    return shared


def build(n_layers=DEPTH, final=True, stop_after=None):
    nc = bass.Bass("TRN2", target_bir_lowering=False)
    dram = {}

    def din(name, shape, dt=F32):
        dram[name] = nc.dram_tensor(name, list(shape), dt, kind="ExternalInput").ap()
        return dram[name]

    def dint(name, shape, dt):
        dram[name] = nc.dram_tensor(name, list(shape), dt, kind="Internal").ap()
        return dram[name]

    xin = din("xT", [D, S])
    outT = nc.dram_tensor("outT", [D, S], F32, kind="ExternalOutput").ap()
    din("pvec", [128, PV_COLS])
    din("cosT", [128, S])
    din("sinT", [128, S])
    din("masks", [128, 2, 128])
    wlist = []
    for l in range(DEPTH):
        j = l // 2
        specs = (WSPEC_EVEN if l % 2 == 0 else WSPEC_ODD)
        for nm, n in specs:
            wlist.append((l, "%s%d" % (nm, j), n))
        for nm, n in WSPEC_FFN:
            wlist.append((l, "%s%d" % (nm, l), n))
    for l, nm, n in wlist:
        din(nm, [n])
        dint(nm + "_b", [n], BF16)
    for j in range(2):
        din("lng%d" % j, [1, 1024])
        din("lnb%d" % j, [1, 1024])
        din("sgb%d" % j, [1, 1024])
        din("sink%d" % j, [1, 16])
    xres = dint("xres", [D, S], F32)
    XA = dint("XA", [D, S], F32)
    GA = dint("GA", [D, S], BF16)
    YA = dint("YA", [D, S], BF16)
    YB = dint("YB", [D, S], BF16)
    QT = dint("QT", [D, S], BF16)
    KT = dint("KT", [256, S], BF16)
    VV = dint("VV", [S, 256], BF16)

    st = ExitStack()
    P = Prog(nc)
    BASE = 24 * 1024
    LIMIT = 204 * 1024
    pa = SBAlloc(nc, BASE, LIMIT)
    pvec = pa.tile("pvec", [128, PV_COLS], F32)
    ones = pa.tile("ones", [128, 128], BF16)
    dum_a = pa.tile("dum_a", [128, 8], F32)
    dum_v = pa.tile("dum_v", [128, 8], F32)
    dum_p = pa.tile("dum_p", [128, 8], F32)
    negc = pa.tile("negc", [128, 2, 16], F32)
    neg2c = pa.tile("neg2c", [128, 2, 16], F32)
    sptmp = pa.tile("sptmp", [128, 16], F32)
    PH_BASE = (pa.off + 1023) // 1024 * 1024
    dummies = (dum_a.t, dum_v.t, dum_p.t)
    banks = [T(nc.alloc_psum_tensor("bank%d" % i, [128, 512], F32), "bank%d" % i) for i in range(8)]

    wbuf = {}
    op_dma(P, "sp", pvec.t[:], dram["pvec"], [], [pvec.b], pvec.b)
    op_memset(P, "pool", ones.t[:], 1.0, [ones.b])
    op_memset(P, "pool", dum_p.t[:], 0.0, [dum_p.b])
    op_memset(P, "dve", dum_v.t[:], 0.0, [dum_v.b])
    op_memset(P, "dve", dum_a.t[:], 0.0, [dum_a.b])
    cast_ops = []
    cast_q = []
    for l, nm, n in wlist:
        if l >= n_layers or stop_after in ('pro0', 'pro2'):
            continue
        b = Buf("w_" + nm)
        wbuf[nm] = b
        src = dram[nm].rearrange("(r f) -> r f", f=2048)
        dst = dram[nm + "_b"].rearrange("(r f) -> r f", f=2048)
        R = n // 2048
        r0 = 0
        while r0 < R:
            r1 = min(R, r0 + 512)
            cast_q.append((nm, dst[r0:r1, :], src[r0:r1, :], b))
            r0 = r1

    def pump(n):
        for _ in range(n):
            if not cast_q:
                return
            nm, d_, s_, b = cast_q.pop(0)
            aft = [cast_ops[-1]] if cast_ops else []
            cast_ops.append(op_dma(P, "pool", d_, s_, [], [b], b, nobar=True, after=aft))

    def ensure(names):
        while cast_q and any(c[0] in names for c in cast_q):
            pump(1)

    for j in range(0 if stop_after in ('pro0', 'pro1') else 2):
        lam = pvec.t[:, PV_EVEN + j * PV_EVEN_SZ + 72: PV_EVEN + j * PV_EVEN_SZ + 88]
        op_act(P, sptmp.t[:], lam, AF.Exp, [pvec.b], [sptmp.b], scale=-1.0)
        op_act(P, sptmp.t[:], sptmp.t[:], AF.Ln, [sptmp.b], [sptmp.b], bias=1.0)
        for d in range(2):
            op_ts(P, "dve", negc.t[:, j, d:16:2], sptmp.t[:, d * 8:(d + 1) * 8], -8.0, None, ALU.mult, None, [sptmp.b], [negc.b])
            op_ts(P, "dve", neg2c.t[:, j, d:16:2], sptmp.t[:, d * 8:(d + 1) * 8], -16.0, None, ALU.mult, None, [sptmp.b], [neg2c.b])

    def xtile_ap(dr, i):
        return dr[:, i * TT:(i + 1) * TT].rearrange("(c p) t -> p c t", p=128)

    def rmsnorm(xt, hn, sqr, lnt, rstd, ncol, bank):
        for c in range(KC):
            s = sqr[c % 2]
            op_act(P, s.t[:], xt.t[:, c, :], AF.Square, [xt.b], [s.b])
            op_mm(P, bank.t[:], ones.t[:], s.t[:], c == 0, c == KC - 1, [ones.b, s.b], [bank.b])
        op_act(P, lnt.t[:], bank.t[:], AF.Ln, [bank.b], [lnt.b], scale=1.0 / D, bias=EPS)
        op_act(P, rstd.t[:], lnt.t[:], AF.Exp, [lnt.b], [rstd.b], scale=-0.5)
        for c in range(KC):
            op_stt(P, hn.t[:, c, :], xt.t[:, c, :], pvec.t[:, ncol + c: ncol + c + 1], rstd.t[:], ALU.mult, ALU.mult,
                   [xt.b, pvec.b, rstd.b], [hn.b])

    class WRing:
        def __init__(self, slots):
            self.slots = slots
            self.i = 0

        def load(self, wname, F, c0, n):
            s = self.slots[self.i % len(self.slots)]
            self.i += 1
            src = dram[wname + "_b"].rearrange("(c p f) -> c p f", p=128, f=F)[c0:c0 + n].rearrange("n p f -> p n f")
            flat = s.t[:, 0:n * F].rearrange("p (n f) -> p n f", f=F)
            op_dma(P, "sp", flat, src, [wbuf[wname]], [s.b], s.b)
            return s

    def ffn_and_store(l, xt, al, ring, i, last):
        hn, sqr, lnt, rstd, act, sg = al["hn"], al["sqr"], al["lnt"], al["rstd"], al["act"], al["sg"]
        rmsnorm(xt, hn, sqr, lnt, rstd, PV_NORM + 16 * l + 8, banks[0])
        wn = "wgu%d" % l
        for c in range(FC):
            if c % 2 == 0:
                w = ring.load(wn, 2048, c, 2)
            wv = w.t[:, (c % 2) * 2048:(c % 2 + 1) * 2048].rearrange("p (g k j) -> p g k j", g=2, k=KC)
            bg = banks[1 + (c % 2)]
            bu = banks[3 + (c % 2)]
            for k in range(KC):
                op_mm(P, bg.t[:], wv[:, 0, k, :], hn.t[:, k, :], k == 0, k == KC - 1, [w.b, hn.b], [bg.b])
            for k in range(KC):
                op_mm(P, bu.t[:], wv[:, 1, k, :], hn.t[:, k, :], k == 0, k == KC - 1, [w.b, hn.b], [bu.b])
            s = sg[c % 2]
            op_act(P, s.t[:], bg.t[:], AF.Silu, [bg.b], [s.b])
            op_tt(P, "dve", act.t[:, c, :], bu.t[:], s.t[:], ALU.mult, [bu.b, s.b], [act.b])
        wn = "wdn%d" % l
        for oc in range(KC):
            w = ring.load(wn, DFF, oc, 1)
            bo = banks[5 + (oc % 2)]
            for k in range(FC):
                op_mm(P, bo.t[:], w.t[:, k * 128:(k + 1) * 128], act.t[:, k, :], k == 0, k == FC - 1, [w.b, act.b], [bo.b])
            op_tt(P, "dve", xt.t[:, oc, :], bo.t[:], xt.t[:, oc, :], ALU.add, [bo.b, xt.b], [xt.b])
        if last:
            for c in range(KC):
                s = sqr[c % 2]
                op_act(P, s.t[:], xt.t[:, c, :], AF.Square, [xt.b], [s.b])
                op_mm(P, banks[0].t[:], ones.t[:], s.t[:], c == 0, c == KC - 1, [ones.b, s.b], [banks[0].b])
            op_act(P, lnt.t[:], banks[0].t[:], AF.Ln, [banks[0].b], [lnt.b], scale=1.0 / D, bias=EPS)
            op_act(P, rstd.t[:], lnt.t[:], AF.Exp, [lnt.b], [rstd.b], scale=-0.5)
            for c in range(KC):
                op_stt(P, xt.t[:, c, :], xt.t[:, c, :], pvec.t[:, 64 + c: 65 + c], rstd.t[:], ALU.mult, ALU.mult,
                       [xt.b, pvec.b, rstd.b], [xt.b])
            op_dma(P, "act", xtile_ap(outT, i), xt.t[:], [xt.b], [], xt.b)
        else:
            op_dma(P, "act", xtile_ap(xres, i), xt.t[:], [xt.b], [], xt.b)

    def common_tiles(a, nslots):
        al = {}
        al["xt"] = [a.tile("xt%d" % k, [128, KC, TT], F32) for k in range(2)]
        al["hn"] = a.tile("hn", [128, KC, TT], BF16)
        al["sqr"] = [a.tile("sqr%d" % k, [128, TT], BF16) for k in range(2)]
        al["lnt"] = a.tile("lnt", [128, TT], F32)
        al["rstd"] = a.tile("rstd", [128, TT], F32)
        al["ring"] = WRing([a.tile("ws%d" % k, [128, 4096], BF16) for k in range(nslots)])
        return al

    def proj_ffn_phase(l, xsrc, wproj, nky, ysrcs):
        a = SBAlloc(nc, PH_BASE, LIMIT)
        xts = [a.tile("xt%d" % k, [128, KC, TT], F32) for k in range(3)]
        yts = [a.tile("yt%d" % k, [128, nky, TT], BF16) for k in range(2)]
        hn = a.tile("hn", [128, KC, TT], BF16)
        sqr = [a.tile("sqr%d" % k, [128, TT], BF16) for k in range(2)]
        lnt = a.tile("lnt", [128, TT], F32)
        rstd = a.tile("rstd", [128, TT], F32)
        ring = WRing([a.tile("ws%d" % k, [128, 4096], BF16) for k in range(7)])
        act = a.tile("act", [128, FC, TT], BF16)
        sg = [a.tile("sg%d" % k, [128, TT], F32) for k in range(2)]
        last_layer = final and l == n_layers - 1
        ncol = PV_NORM + 16 * l + 8
        Fw = nky * 128
        per = 4096 // Fw

        def load(i):
            xt, y = xts[i % 3], yts[i % 2]
            op_dma(P, "sp", xt.t[:], xtile_ap(xsrc, i), [], [xt.b], xt.b)
            fns = []
            k0 = 0
            for (ysrc, nk) in ysrcs:
                fns.append((y.t[:, k0:k0 + nk, :], xtile_ap(ysrc, i)))
                k0 += nk
            P.add("sp", lambda e: [e.dma_start(out=o_, in_=i_) for (o_, i_) in fns], writes=[y.b], owner=y.b, ndma=len(fns))

        def outproj(i):
            xt, y = xts[i % 3], yts[i % 2]
            for oc in range(KC):
                if oc % per == 0:
                    w = ring.load(wproj, Fw, oc, per)
                bk = banks[7 - (oc % 2)]
                o_ = (oc % per) * Fw
                for k in range(nky):
                    op_mm(P, bk.t[:], w.t[:, o_ + k * 128:o_ + (k + 1) * 128], y.t[:, k, :], k == 0, k == nky - 1, [w.b, y.b], [bk.b])
                op_tt(P, "dve", xt.t[:, oc, :], bk.t[:], xt.t[:, oc, :], ALU.add, [bk.b, xt.b], [xt.b])

        def norm(i, col, dst_hn):
            xt = xts[i % 3]
            for c in range(KC):
                s = sqr[c % 2]
                op_act(P, s.t[:], xt.t[:, c, :], AF.Square, [xt.b], [s.b])
                op_mm(P, banks[0].t[:], ones.t[:], s.t[:], c == 0, c == KC - 1, [ones.b, s.b], [banks[0].b])
            op_act(P, lnt.t[:], banks[0].t[:], AF.Ln, [banks[0].b], [lnt.b], scale=1.0 / D, bias=EPS)
            op_act(P, rstd.t[:], lnt.t[:], AF.Exp, [lnt.b], [rstd.b], scale=-0.5)
            for c in range(KC):
                if dst_hn:
                    op_stt(P, hn.t[:, c, :], xt.t[:, c, :], pvec.t[:, col + c: col + c + 1], rstd.t[:], ALU.mult, ALU.mult,
                           [xt.b, pvec.b, rstd.b], [hn.b])
                else:
                    op_stt(P, xt.t[:, c, :], xt.t[:, c, :], pvec.t[:, col + c: col + c + 1], rstd.t[:], ALU.mult, ALU.mult,
                           [xt.b, pvec.b, rstd.b], [xt.b])

        def gate_up(i):
            wn = "wgu%d" % l
            for c in range(FC):
                if c % 2 == 0:
                    w = ring.load(wn, 2048, c, 2)
                wv = w.t[:, (c % 2) * 2048:(c % 2 + 1) * 2048].rearrange("p (g k j) -> p g k j", g=2, k=KC)
                bg = banks[1 + (c % 2)]
                bu = banks[3 + (c % 2)]
                for k in range(KC):
                    op_mm(P, bg.t[:], wv[:, 0, k, :], hn.t[:, k, :], k == 0, k == KC - 1, [w.b, hn.b], [bg.b])
                for k in range(KC):
                    op_mm(P, bu.t[:], wv[:, 1, k, :], hn.t[:, k, :], k == 0, k == KC - 1, [w.b, hn.b], [bu.b])
                s = sg[c % 2]
                op_act(P, s.t[:], bg.t[:], AF.Silu, [bg.b], [s.b])
                op_tt(P, "dve", act.t[:, c, :], bu.t[:], s.t[:], ALU.mult, [bu.b, s.b], [act.b])

        def down(i):
            xt = xts[i % 3]
            wn = "wdn%d" % l
            for oc in range(KC):
                w = ring.load(wn, DFF, oc, 1)
                bo = banks[5 + (oc % 2)]
                for k in range(FC):
                    op_mm(P, bo.t[:], w.t[:, k * 128:(k + 1) * 128], act.t[:, k, :], k == 0, k == FC - 1, [w.b, act.b], [bo.b])
                op_tt(P, "dve", xt.t[:, oc, :], bo.t[:], xt.t[:, oc, :], ALU.add, [bo.b, xt.b], [xt.b])

        load(0)
        load(1)
        outproj(0)
        norm(0, ncol, True)
        for i in range(NT):
            xt = xts[i % 3]
            if i + 2 < NT:
                load(i + 2)
            pump(2)
            if i + 1 < NT:
                outproj(i + 1)
            gate_up(i)
            if i + 1 < NT and not last_layer:
                norm(i + 1, ncol, True)
            down(i)
            if last_layer:
                norm(i, 64, False)
                op_dma(P, "act", xtile_ap(outT, i), xt.t[:], [xt.b], [], xt.b)
                if i + 1 < NT:
                    norm(i + 1, ncol, True)
            else:
                op_dma(P, "act", xtile_ap(xres, i), xt.t[:], [xt.b], [], xt.b)
        P.barrier(dummies)

    def even_layer(l, xsrc):
        j = l // 2
        pb = PV_EVEN + j * PV_EVEN_SZ
        ensure(["win_fm%d" % j, "win_v%d" % j, "sgwT%d" % j, "gates%d" % j])
        a = SBAlloc(nc, PH_BASE, LIMIT)
        al = common_tiles(a, 3)
        ring = al["ring"]
        hn = al["hn"]
        XAs = a.tile("XAs", [128, KC, TT], F32)
        GAs = a.tile("GAs", [128, KC, TT], BF16)
        YBs = a.tile("YBs", [128, KC, TT], BF16)
        tsv = [a.tile("tsv%d" % k, [128, TT], F32) for k in range(2)]
        U = a.tile("U", [128, KC, TT], BF16)
        G = a.tile("G", [128, 4, 1024], F32)
        V = a.tile("V", [128, 4, 1024], BF16)
        lng = a.tile("lng", [128, 1024], F32)
        lnb = a.tile("lnb", [128, 1024], F32)
        sgb = a.tile("sgb", [128, 1024], F32)
        sgw = a.tile("sgw", [128, 1024], BF16)
        wv = a.tile("wv", [128, KC, 1024], BF16)
        stats = a.tile("stats", [128, 4, 2, 6], F32)
        mv = a.tile("mv", [128, 4, 2], F32)
        lrs = a.tile("lrs", [128, 4], F32)
        op_dma(P, "sp", lng.t[:], dram["lng%d" % j].partition_broadcast(128), [], [lng.b], lng.b)
        op_dma(P, "sp", lnb.t[:], dram["lnb%d" % j].partition_broadcast(128), [], [lnb.b], lnb.b)
        op_dma(P, "sp", sgb.t[:], dram["sgb%d" % j].partition_broadcast(128), [], [sgb.b], sgb.b)
        op_dma(P, "sp", sgw.t[:], dram["sgwT%d_b" % j].rearrange("(q f) -> q f", f=1024), [wbuf["sgwT%d" % j]], [sgw.b], sgw.b)
        op_dma(P, "sp", wv.t[:], dram["win_v%d_b" % j].rearrange("(k p c) -> p k c", p=128, c=1024), [wbuf["win_v%d" % j]], [wv.b], wv.b)

        def e1_tail(ti):
            for g in range(8):
                bk = banks[6 + (g % 2)]
                for sidx in range(4):
                    op_mm(P, bk.t[:, sidx * 128:(sidx + 1) * 128], V.t[:, sidx, g * 128:(g + 1) * 128], sgw.t[:, g * 128:(g + 1) * 128],
                          True, True, [V.b, sgw.b], [bk.b])
                ts_ = tsv[g % 2]
                op_tt(P, "dve", ts_.t[:].rearrange("p (s q) -> p s q", s=4), bk.t[:].rearrange("p (s q) -> p s q", s=4),
                      sgb.t[:, g * 128:(g + 1) * 128].unsqueeze(1).to_broadcast([128, 4, 128]), ALU.add, [bk.b, sgb.b], [ts_.b])
                op_tt(P, "dve", YBs.t[:, g, :], ts_.t[:], U.t[:, g, :], ALU.mult, [ts_.b, U.b], [YBs.b])

        op_dma(P, "sp", al["xt"][0].t[:], xtile_ap(xsrc, 0), [], [al["xt"][0].b], al["xt"][0].b)
        for i in range(NT):
            xt = al["xt"][i % 2]
            if i + 1 < NT:
                xn = al["xt"][(i + 1) % 2]
                op_dma(P, "sp", xn.t[:], xtile_ap(xsrc, i + 1), [], [xn.b], xn.b)
            pump(2)
            rmsnorm(xt, hn, al["sqr"], al["lnt"], al["rstd"], PV_NORM + 16 * l, banks[0])
            for oc in range(24):
                if oc % 4 == 0:
                    w = ring.load("win_fm%d" % j, 1024, oc, 4)
                bk = banks[1 + (oc % 3)]
                for k in range(KC):
                    op_mm(P, bk.t[:], w.t[:, (oc % 4) * 1024 + k * 128:(oc % 4) * 1024 + (k + 1) * 128], hn.t[:, k, :],
                          k == 0, k == KC - 1, [w.b, hn.b], [bk.b])
                c = oc % 8
                rows = slice(c * 128, (c + 1) * 128)
                cols = slice(i * TT, (i + 1) * TT)
                if oc < 8:
                    op_act(P, XAs.t[:, c, :], bk.t[:], AF.Copy, [bk.b], [XAs.b])
                    if oc == 7:
                        op_dma(P, "act", xtile_ap(XA, i), XAs.t[:], [XAs.b], [], XAs.b)
                elif oc < 16:
                    op_act(P, GAs.t[:, c, :], bk.t[:], AF.Gelu_apprx_tanh, [bk.b], [GAs.b])
                    if oc == 15:
                        op_dma(P, "act", xtile_ap(GA, i), GAs.t[:], [GAs.b], [], GAs.b)
                        if i > 0:
                            e1_tail(i - 1)
                else:
                    op_act(P, U.t[:, c, :], bk.t[:], AF.Gelu_apprx_tanh, [bk.b], [U.b])
            for sidx in range(4):
                for hh in range(2):
                    bk = banks[4 + hh]
                    for k in range(KC):
                        op_mm(P, bk.t[:], hn.t[:, k, sidx * 128:(sidx + 1) * 128], wv.t[:, k, hh * 512:(hh + 1) * 512],
                              k == 0, k == KC - 1, [hn.b, wv.b], [bk.b])
                    op_act(P, G.t[:, sidx, hh * 512:(hh + 1) * 512], bk.t[:], AF.Gelu_apprx_tanh, [bk.b], [G.b])
                    P.add("dve", (lambda e, o=stats.t[:, sidx, hh, :], i_=G.t[:, sidx, hh * 512:(hh + 1) * 512]: e.bn_stats(out=o, in_=i_)),
                          reads=[G.b], writes=[stats.b])
                P.add("dve", (lambda e, o=mv.t[:, sidx, :], i_=stats.t[:, sidx, :, :].rearrange("p a b -> p (a b)"): e.bn_aggr(out=o, in_=i_)),
                      reads=[stats.b], writes=[mv.b])
            op_act(P, lrs.t[:], mv.t[:, :, 1], AF.Ln, [mv.b], [lrs.b], bias=EPS)
            op_act(P, lrs.t[:], lrs.t[:], AF.Exp, [lrs.b], [lrs.b], scale=-0.5)
            for sidx in range(4):
                op_ts(P, "dve", G.t[:, sidx, :], G.t[:, sidx, :], mv.t[:, sidx, 0:1], lrs.t[:, sidx:sidx + 1], ALU.subtract, ALU.mult,
                      [G.b, mv.b, lrs.b], [G.b])
                op_tt(P, "pool", G.t[:, sidx, :], G.t[:, sidx, :], lng.t[:], ALU.mult, [G.b, lng.b], [G.b])
                op_tt(P, "pool", V.t[:, sidx, :], G.t[:, sidx, :], lnb.t[:], ALU.add, [G.b, lnb.b], [V.b])
            if i > 0:
                op_dma(P, "act", xtile_ap(YB, i - 1), YBs.t[:], [YBs.b], [], YBs.b)
        e1_tail(NT - 1)
        op_dma(P, "act", xtile_ap(YB, NT - 1), YBs.t[:], [YBs.b], [], YBs.b)
        P.barrier(dummies)
        if stop_after == 'E1':
            return

        a = SBAlloc(nc, PH_BASE, LIMIT)
        xa = a.tile("xa", [128, S + 4], F32)
        xc = [a.tile("xc%d" % k, [128, S], F32) for k in range(2)]
        xcb = [a.tile("xcb%d" % k, [128, S], BF16) for k in range(2)]
        rr = [a.tile("rr%d" % k, [128, TT], F32) for k in range(NT)]
        ii = [a.tile("ii%d" % k, [128, TT], F32) for k in range(NT)]
        t1 = [a.tile("t1%d" % k, [128, TT], F32) for k in range(NT)]
        hd = [a.tile("hd%d" % k, [128, S], F32) for k in range(2)]
        gat = [a.tile("gat%d" % k, [128, S], BF16) for k in range(2)]
        ya = a.tile("ya", [128, S], BF16)
        gw = [a.tile("gw%d" % k, [128, 4, 128], BF16) for k in range(2)]
        op_memset(P, "pool", xa.t[:, 0:2], 0.0, [xa.b])
        op_memset(P, "pool", xa.t[:, S + 2:S + 4], 0.0, [xa.b])
        gsrc = dram["gates%d_b" % j].rearrange("(h p f) -> h p f", p=128, f=512)

        def e2_load(h):
            rows = slice(h * 128, (h + 1) * 128)
            g_ = gw[h % 2]
            op_dma(P, "sp", g_.t[:].rearrange("p a b -> p (a b)"), gsrc[h], [wbuf["gates%d" % j]], [g_.b], g_.b)
            op_dma(P, "sp", xa.t[:, 2:S + 2], XA[rows, :], [], [xa.b], xa.b)
            op_dma(P, "sp", gat[h % 2].t[:], GA[rows, :], [], [gat[h % 2].b], gat[h % 2].b)

        def e2_conv(h):
            xc_ = xc[h % 2]
            cwc = pb + h * 4
            op_ts(P, "dve", xc_.t[:], xa.t[:, 0:S], pvec.t[:, cwc:cwc + 1], pvec.t[:, pb + 32 + h: pb + 33 + h], ALU.mult, ALU.add,
                  [xa.b, pvec.b], [xc_.b])
            for k in range(1, 4):
                op_stt(P, xc_.t[:], xa.t[:, k:k + S], pvec.t[:, cwc + k: cwc + k + 1], xc_.t[:], ALU.mult, ALU.add,
                       [xa.b, pvec.b, xc_.b], [xc_.b])
            op_act(P, xcb[h % 2].t[:], xc_.t[:], AF.Copy, [xc_.b], [xcb[h % 2].b])

        e2_load(0)
        e2_conv(0)
        for h in range(8):
            rows = slice(h * 128, (h + 1) * 128)
            g_ = gw[h % 2]
            xc_ = xc[h % 2]
            xcb_ = xcb[h % 2]
            if h + 1 < 8:
                e2_load(h + 1)
            pump(1)
            for d in range(2):
                order = list(range(NT)) if d == 0 else list(range(NT - 1, -1, -1))
                cr = pb + 40 + 16 * d + h
                for t in order:
                    cs = slice(t * TT, (t + 1) * TT)
                    br = banks[(t % 2) * 2]
                    bi = banks[(t % 2) * 2 + 1]
                    op_mm(P, br.t[:], g_.t[:, 2 * d, :], xcb_.t[:, cs], True, True, [g_.b, xcb_.b], [br.b])
                    op_mm(P, bi.t[:], g_.t[:, 2 * d + 1, :], xcb_.t[:, cs], True, True, [g_.b, xcb_.b], [bi.b])
                    op_act(P, rr[t].t[:], br.t[:], AF.Sigmoid, [br.b, pvec.b], [rr[t].b], bias=pvec.t[:, cr:cr + 1])
                    op_act(P, ii[t].t[:], bi.t[:], AF.Sigmoid, [bi.b, pvec.b], [ii[t].b], bias=pvec.t[:, cr + 8:cr + 9])
                sc = negc.t[:, j, 2 * h + d: 2 * h + d + 1]
                sc2 = neg2c.t[:, j, 2 * h + d: 2 * h + d + 1]
                for t in order:
                    op_act(P, t1[t].t[:], rr[t].t[:], AF.Exp, [rr[t].b, neg2c.b], [t1[t].b], scale=sc2)
                    op_act(P, rr[t].t[:], rr[t].t[:], AF.Exp, [rr[t].b, negc.b], [rr[t].b], scale=sc)
                    op_act(P, t1[t].t[:], t1[t].t[:], AF.Ln, [t1[t].b], [t1[t].b], scale=-1.0, bias=1.0)
                    op_act(P, t1[t].t[:], t1[t].t[:], AF.Exp, [t1[t].b], [t1[t].b], scale=0.5)
                hh_ = hd[d]
                for n_, t in enumerate(order):
                    cs = slice(t * TT, (t + 1) * TT)
                    op_tt(P, "dve", ii[t].t[:], ii[t].t[:], xc_.t[:, cs], ALU.mult, [ii[t].b, xc_.b], [ii[t].b])
                    op_tt(P, "dve", ii[t].t[:], ii[t].t[:], t1[t].t[:], ALU.mult, [ii[t].b, t1[t].b], [ii[t].b])
                    if d == 0:
                        init = 0.0 if n_ == 0 else hh_.t[:, t * TT - 1:t * TT]
                        P.add("dve", (lambda e, o=hh_.t[:, cs], a_=rr[t].t[:], u_=ii[t].t[:], i_=init: e.tensor_tensor_scan(out=o, data0=a_, data1=u_, initial=i_, op0=ALU.mult, op1=ALU.add)),
                              reads=[rr[t].b, ii[t].b, hh_.b], writes=[hh_.b])
                    else:
                        init = 0.0 if n_ == 0 else hh_.t[:, (t + 1) * TT:(t + 1) * TT + 1]
                        P.add("dve", (lambda e, o=hh_.t[:, t * TT:(t + 1) * TT][:, ::-1], a_=rr[t].t[:, ::-1], u_=ii[t].t[:, ::-1], i_=init: e.tensor_tensor_scan(out=o, data0=a_, data1=u_, initial=i_, op0=ALU.mult, op1=ALU.add)),
                              reads=[rr[t].b, ii[t].b, hh_.b], writes=[hh_.b])
                if d == 0 and h + 1 < 8:
                    e2_conv(h + 1)
            op_tt(P, "dve", hd[0].t[:], hd[0].t[:], hd[1].t[:], ALU.add, [hd[0].b, hd[1].b], [hd[0].b])
            op_tt(P, "dve", ya.t[:], hd[0].t[:], gat[h % 2].t[:], ALU.mult, [hd[0].b, gat[h % 2].b], [ya.b])
            op_dma(P, "sp", YA[rows, :], ya.t[:], [ya.b], [], ya.b)
        P.barrier(dummies)
        if stop_after == 'E2':
            return

        ensure(["wout%d" % j, "wgu%d" % l, "wdn%d" % l])
        proj_ffn_phase(l, xsrc, "wout%d" % j, 2 * KC, [(YA, KC), (YB, KC)])

    def odd_layer(l, xsrc):
        j = l // 2
        ensure(["wqk%d" % j, "wv%d" % j])
        a = SBAlloc(nc, PH_BASE, LIMIT)
        al = common_tiles(a, 4)
        ring = al["ring"]
        hn = al["hn"]
        cs_t = [a.tile("cos%d" % k, [128, TT], F32) for k in range(2)]
        sn_t = [a.tile("sin%d" % k, [128, TT], F32) for k in range(2)]
        tA = [a.tile("tA%d" % k, [128, TT], F32) for k in range(2)]
        tB = [a.tile("tB%d" % k, [128, TT], F32) for k in range(2)]
        Qs = a.tile("Qs", [128, KC, TT], BF16)
        Ks = a.tile("Ks", [128, 2, TT], BF16)
        Vs = a.tile("Vs", [128, 4, 256], BF16)
        wv = a.tile("wv", [128, KC, 256], BF16)
        op_dma(P, "sp", wv.t[:], dram["wv%d_b" % j].rearrange("(k p c) -> p k c", p=128, c=256), [wbuf["wv%d" % j]], [wv.b], wv.b)

        def load_tile(i):
            xt = al["xt"][i % 2]
            op_dma(P, "sp", xt.t[:], xtile_ap(xsrc, i), [], [xt.b], xt.b)
            op_dma(P, "sp", cs_t[i % 2].t[:], dram["cosT"][:, i * TT:(i + 1) * TT], [], [cs_t[i % 2].b], cs_t[i % 2].b)
            op_dma(P, "sp", sn_t[i % 2].t[:], dram["sinT"][:, i * TT:(i + 1) * TT], [], [sn_t[i % 2].b], sn_t[i % 2].b)
        load_tile(0)
        cnt = 0
        for i in range(NT):
            xt = al["xt"][i % 2]
            cs_, sn_ = cs_t[i % 2], sn_t[i % 2]
            if i + 1 < NT:
                load_tile(i + 1)
            pump(2)
            rmsnorm(xt, hn, al["sqr"], al["lnt"], al["rstd"], PV_NORM + 16 * l, banks[0])
            for grp, (c0, nchunk, dst) in enumerate([(0, 8, QT), (16, 2, KT)]):
                for cc in range(nchunk):
                    if cc % 2 == 0:
                        n_ = min(2, nchunk - cc)
                        wN = ring.load("wqk%d" % j, 1024, c0 + cc, n_)
                        wS = ring.load("wqk%d" % j, 1024, c0 + nchunk + cc, n_)
                    bn_ = banks[1 + (cnt % 2) * 2]
                    bs_ = banks[2 + (cnt % 2) * 2]
                    o_ = (cc % 2) * 1024
                    for k in range(KC):
                        op_mm(P, bn_.t[:], wN.t[:, o_ + k * 128:o_ + (k + 1) * 128], hn.t[:, k, :], k == 0, k == KC - 1, [wN.b, hn.b], [bn_.b])
                    for k in range(KC):
                        op_mm(P, bs_.t[:], wS.t[:, o_ + k * 128:o_ + (k + 1) * 128], hn.t[:, k, :], k == 0, k == KC - 1, [wS.b, hn.b], [bs_.b])
                    ta, tb = tA[cnt % 2], tB[cnt % 2]
                    stg_ = Qs if grp == 0 else Ks
                    op_tt(P, "dve", ta.t[:], bn_.t[:], cs_.t[:], ALU.mult, [bn_.b, cs_.b], [ta.b])
                    op_tt(P, "dve", tb.t[:], bs_.t[:], sn_.t[:], ALU.mult, [bs_.b, sn_.b], [tb.b])
                    op_tt(P, "pool", stg_.t[:, cc, :], ta.t[:], tb.t[:], ALU.add, [ta.b, tb.b], [stg_.b])
                    cnt += 1
                if grp == 0:
                    op_dma(P, "act", xtile_ap(QT, i), Qs.t[:], [Qs.b], [], Qs.b)
                else:
                    op_dma(P, "act", xtile_ap(KT, i), Ks.t[:], [Ks.b], [], Ks.b)
            for sidx in range(4):
                bk = banks[5 + (sidx % 2)]
                for k in range(KC):
                    op_mm(P, bk.t[:, 0:256], hn.t[:, k, sidx * 128:(sidx + 1) * 128], wv.t[:, k, :], k == 0, k == KC - 1, [hn.b, wv.b], [bk.b])
                op_act(P, Vs.t[:, sidx, :], bk.t[:, 0:256], AF.Copy, [bk.b], [Vs.b])
            op_dma(P, "act", VV[i * TT:(i + 1) * TT, :].rearrange("(s p) c -> p s c", p=128), Vs.t[:], [Vs.b], [], Vs.b)
        P.barrier(dummies)
        if stop_after == 'O1':
            return

        a = SBAlloc(nc, PH_BASE, LIMIT)
        NB = S // 128
        kt = [a.tile("kt%d" % k, [128, S], BF16) for k in range(2)]
        qt = [a.tile("qt%d" % k, [128, 4, S], BF16) for k in range(2)]
        vt = [a.tile("vt%d" % k, [128, NB, 128], BF16) for k in range(2)]
        ot = a.tile("ot", [64, 4, S], BF16)
        NE = 8
        et = [a.tile("et%d" % k, [128, 4, 128], BF16) for k in range(NE)]
        dt_ = [a.tile("dt%d" % k, [64, 4, 128], F32) for k in range(4)]
        mk32 = a.tile("mk32", [128, 2, 128], F32)
        mk = a.tile("mk", [128, 2, 128], BF16)
        snk = a.tile("snk", [128, 16], F32)
        esb = a.tile("esb", [128, 16, 128], F32)
        for k in range(2):
            op_memset(P, "pool", kt[k].t[64:128, :], 0.0, [kt[k].b])
            op_memset(P, "pool", qt[k].t[64:128, :, :], 0.0, [qt[k].b])
            op_memset(P, "pool", vt[k].t[:, :, 64:128], 1.0, [vt[k].b])
        op_dma(P, "sp", mk32.t[:], dram["masks"], [], [mk32.b], mk32.b)
        op_copy(P, "dve", mk.t[:], mk32.t[:], [mk32.b], [mk.b])
        op_dma(P, "sp", snk.t[:], dram["sink%d" % j].partition_broadcast(128), [], [snk.b], snk.b)
        op_act(P, snk.t[:], snk.t[:], AF.Exp, [snk.b], [snk.b])
        op_copy(P, "dve", esb.t[:], snk.t[:].unsqueeze(2).to_broadcast([128, 16, 128]), [snk.b], [esb.b])

        def load_g(g):
            k_, q_, v_ = kt[g % 2], qt[g % 2], vt[g % 2]
            op_dma(P, "sp", k_.t[0:64, :], KT[g * 64:(g + 1) * 64, :], [], [k_.b], k_.b)
            op_dma(P, "sp", q_.t[0:64, :, :], QT[g * 256:(g + 1) * 256, :].rearrange("(h d) t -> d h t", d=64), [], [q_.b], q_.b)
            op_dma(P, "sp", v_.t[:, :, 0:64], VV[:, g * 64:(g + 1) * 64].rearrange("(n p) d -> p n d", p=128), [], [v_.b], v_.b)
        load_g(0)
        ctr = {"ec": 0}

        def scores(g, ib):
            k_, q_ = kt[g % 2], qt[g % 2]
            js = [jb for jb in (ib - 1, ib, ib + 1) if 0 <= jb < NB]
            qs = q_.t[:, :, ib * 128:(ib + 1) * 128]
            es = []
            for jb in js:
                ec = ctr["ec"]
                ctr["ec"] += 1
                bs_ = banks[ec % 4]
                e_ = et[ec % NE]
                op_mm(P, bs_.t[:].rearrange("p (h q) -> p h q", h=4), k_.t[:, jb * 128:(jb + 1) * 128], qs, True, True, [k_.b, q_.b], [bs_.b])
                op_act(P, e_.t[:].rearrange("p h q -> p (h q)"), bs_.t[:], AF.Exp, [bs_.b], [e_.b], scale=0.125)
                if jb != ib:
                    mi = 1 if jb > ib else 0
                    op_tt(P, "dve", e_.t[:], e_.t[:], mk.t[:, mi, :].unsqueeze(1).to_broadcast([128, 4, 128]), ALU.mult, [e_.b, mk.b], [e_.b])
                es.append((jb, e_))
            return es

        def pv(g, ib, es):
            v_ = vt[g % 2]
            bo = banks[4 + (ib % 4)]
            for n_, (jb, e_) in enumerate(es):
                ef = e_.t[:].rearrange("p h q -> p (h q)")
                op_mm(P, bo.t[:], v_.t[:, jb, :], ef, n_ == 0, n_ == len(es) - 1, [v_.b, e_.b], [bo.b])
            d_ = dt_[ib % 4]
            op_tt(P, "dve", d_.t[:], bo.t[64:128, :].rearrange("p (h q) -> p h q", h=4), esb.t[64:128, g * 4:(g + 1) * 4, :], ALU.add, [bo.b, esb.b], [d_.b])

        def nrm(g, ib):
            bo = banks[4 + (ib % 4)]
            d_ = dt_[ib % 4]
            df = d_.t[:].rearrange("p h q -> p (h q)")
            op_act(P, df, df, AF.Ln, [d_.b], [d_.b])
            op_act(P, df, df, AF.Exp, [d_.b], [d_.b], scale=-1.0)
            op_tt(P, "dve", ot.t[:, :, ib * 128:(ib + 1) * 128], bo.t[0:64, :].rearrange("p (h q) -> p h q", h=4), d_.t[:], ALU.mult, [bo.b, d_.b], [ot.b])

        for g in range(4):
            if g + 1 < 4:
                load_g(g + 1)
            pump(2)
            prev = None
            for ib in range(NB):
                es = scores(g, ib)
                if prev is not None:
                    pv(g, prev[0], prev[1])
                if ib >= 2:
                    nrm(g, ib - 2)
                prev = (ib, es)
            pv(g, prev[0], prev[1])
            nrm(g, NB - 2)
            nrm(g, NB - 1)
            op_dma(P, "sp", YA[g * 256:(g + 1) * 256, :].rearrange("(h d) t -> d h t", d=64), ot.t[:], [ot.b], [], ot.b)
        P.barrier(dummies)
        if stop_after == 'O2':
            return

        ensure(["wo%d" % j, "wgu%d" % l, "wdn%d" % l])
        proj_ffn_phase(l, xsrc, "wo%d" % j, KC, [(YA, KC)])

    for l in range(n_layers if stop_after not in ('pro', 'pro0', 'pro1', 'pro2') else 0):
        xsrc = xin if l == 0 else xres
        if l % 2 == 0:
            even_layer(l, xsrc)
        else:
            odd_layer(l, xsrc)
    if not final:
        a = SBAlloc(nc, PH_BASE, LIMIT)
        xt = a.tile("xo", [128, KC, TT], F32)
        for i in range(NT):
            op_dma(P, "sp", xt.t[:], xtile_ap(xres, i), [], [xt.b], xt.b)
            op_dma(P, "sp", xtile_ap(outT, i), xt.t[:], [xt.b], [], xt.b)
    P.barrier(dummies)
    P.add("sp", lambda e: e.nop())
    P.emit(st)
    st.close()
    return nc, P


def kernel(**inputs):
    inp = {k: np.asarray(v) for k, v in inputs.items()}
    shared = host_prep(inp)
    nc, _ = build()
    x = inp["x"]
    in_maps = []
    for c in range(NCORES):
        m = dict(shared)
        m["xT"] = np.ascontiguousarray(x[c].T)
        in_maps.append(m)
    res = run_bass_kernel_spmd(nc, in_maps, core_ids=list(range(NCORES)))
    out = np.stack([np.asarray(res.results[c]["outT"]).T for c in range(NCORES)], axis=0)
    return np.ascontiguousarray(out.astype(np.float32))
```

```python
import math
import numpy as np
from contextlib import ExitStack
import concourse.bass as bass
import concourse.mybir as mybir
from concourse.bass_utils import run_bass_kernel_spmd

F32 = mybir.dt.float32
BF16 = mybir.dt.bfloat16
AF = mybir.ActivationFunctionType
ALU = mybir.AluOpType
ENGS = ("pe", "act", "dve", "pool", "sp")

D = 1024
S = 4096
TT = 512
NT = S // TT
KC = D // 128
DFF = 2816
FC = DFF // 128
DEPTH = 4
EPS = 1e-6
NCORES = 8


class Buf:
    __slots__ = ("name", "w", "rd", "rdma", "slot", "semval", "last_dma")

    def __init__(self, name):
        self.name = name
        self.w = None
        self.rd = {}
        self.rdma = []
        self.slot = None
        self.semval = 0
        self.last_dma = None


class Slot:
    __slots__ = ("val", "sem", "idx")

    def __init__(self, idx):
        self.val = 0
        self.sem = None
        self.idx = idx


class Op:
    __slots__ = ("eng", "fn", "deps", "signal", "sem", "sigval", "is_dma", "ndma", "owner")

    def __init__(self, eng, fn, is_dma):
        self.eng = eng
        self.fn = fn
        self.deps = []
        self.signal = False
        self.sem = None
        self.sigval = 0
        self.is_dma = is_dma
        self.ndma = 0
        self.owner = None


class Prog:
    def __init__(self, nc):
        self.nc = nc
        self.ops = {e: [] for e in ENGS}
        self.slots = []
        self.free_slots = []
        self.active = []
        self.pending_dma = []
        self.need_bar = {e: None for e in ENGS}
        self.nops = 0

    def _dep(self, op, d, kind):
        if d is op:
            return
        if (not d.is_dma) and (not op.is_dma) and d.eng == op.eng:
            if op.eng == "pe":
                return
            if kind == "war":
                return
        d.signal = True
        op.deps.append(d)

    def add(self, eng, fn, reads=(), writes=(), owner=None, ndma=0, after=(), nobar=False):
        op = Op(eng, fn, owner is not None)
        if self.need_bar[eng] is not None:
            self._dep(op, self.need_bar[eng], "bar")
            self.need_bar[eng] = None
        for d in after:
            self._dep(op, d, "bar")
        for b in reads:
            if b.w is not None:
                self._dep(op, b.w, "raw")
        for b in writes:
            if b.w is not None:
                self._dep(op, b.w, "waw")
            for r in b.rd.values():
                self._dep(op, r, "war")
            for r in b.rdma:
                self._dep(op, r, "war")
        if owner is not None:
            op.owner = owner
            op.ndma = ndma
            if owner.last_dma is not None:
                self._dep(op, owner.last_dma, "chain")
            owner.last_dma = op
            if owner.slot is None:
                if (not nobar) and self.free_slots:
                    owner.slot = self.free_slots.pop()
                else:
                    owner.slot = Slot(len(self.slots))
                    self.slots.append(owner.slot)
                if not nobar:
                    self.active.append(owner)
            owner.slot.val += 16 * ndma
            op.sigval = owner.slot.val
            op.sem = owner.slot
            if not nobar:
                self.pending_dma.append(op)
        for b in reads:
            if op.is_dma:
                b.rdma.append(op)
            else:
                b.rd[eng] = op
        for b in writes:
            b.w = op
            b.rd = {}
            b.rdma = []
        self.ops[eng].append(op)
        self.nops += 1
        return op

    def barrier(self, dummies):
        da, dv, dp = dummies
        m_act = self.add("act", lambda e: e.activation(out=da[0:1, 0:1], in_=da[0:1, 1:2], func=AF.Copy))
        m_dve = self.add("dve", lambda e: e.memset(dv[0:1, 0:1], 0.0))
        hub = self.add("pool", lambda e: e.memset(dp[0:1, 0:1], 0.0), after=[m_act, m_dve] + self.pending_dma)
        self.pending_dma = []
        for o in self.active:
            self.free_slots.append(o.slot)
            o.slot = None
        self.active = []
        for e in ENGS:
            self.need_bar[e] = hub
        self.need_bar["pool"] = None
        return hub

    def emit(self, stack):
        nc = self.nc
        engsem = {e: stack.enter_context(nc.semaphore("cnt_" + e)) for e in ENGS}
        for sl in self.slots:
            sl.sem = stack.enter_context(nc.semaphore("dsl%d" % sl.idx))
        for e in ENGS:
            c = 0
            for op in self.ops[e]:
                if op.is_dma:
                    op.sem = op.sem.sem
                else:
                    op.sem = engsem[e]
                    if op.signal:
                        c += 1
                        op.sigval = c
        block = stack.enter_context(nc.Block())
        ops = self.ops

        def run(e, eh):
            seen = {}
            for op in ops[e]:
                waits = {}
                for d in op.deps:
                    k = id(d.sem)
                    if seen.get(k, 0) < d.sigval:
                        if k not in waits or waits[k][1] < d.sigval:
                            waits[k] = (d.sem, d.sigval)
                for k, (sem, val) in waits.items():
                    eh.wait_ge(sem, val)
                    seen[k] = val
                ins = op.fn(eh)
                if op.is_dma:
                    assert len(ins) == op.ndma, (len(ins), op.ndma)
                    for i in ins:
                        i.then_inc(op.sem, 16)
                elif op.signal:
                    ins.then_inc(op.sem, 1)

        @block.tensor
        def _(eh):
            run("pe", eh)

        @block.scalar
        def _(eh):
            run("act", eh)

        @block.vector
        def _(eh):
            run("dve", eh)

        @block.gpsimd
        def _(eh):
            run("pool", eh)

        @block.sync
        def _(eh):
            run("sp", eh)


class T:
    __slots__ = ("t", "b")

    def __init__(self, t, name):
        self.t = t
        self.b = Buf(name)


class SBAlloc:
    def __init__(self, nc, base, limit):
        self.nc = nc
        self.off = base
        self.limit = limit
        self.n = 0

    def tile(self, name, shape, dtype):
        esz = 4 if dtype == F32 else 2
        nbytes = esz
        for s in shape[1:]:
            nbytes *= s
        nbytes = (nbytes + 31) // 32 * 32
        assert self.off + nbytes <= self.limit, ("SBUF overflow", name, self.off, nbytes)
        self.n += 1
        t = self.nc.alloc_sbuf_tensor_at("%s_%d" % (name, self.n), list(shape), dtype, offset=self.off)
        self.off += nbytes
        return T(t, name)


def op_dma(P, eng, out_ap, in_ap, reads, writes, owner, **kw):
    return P.add(eng, lambda e: [e.dma_start(out=out_ap, in_=in_ap)], reads=reads, writes=writes, owner=owner, ndma=1, **kw)


def op_mm(P, ps_ap, lhsT, rhs, start, stop, reads, writes):
    return P.add("pe", lambda e: e.matmul(ps_ap, lhsT=lhsT, rhs=rhs, start=start, stop=stop), reads=reads, writes=writes)


def op_act(P, out, in_, func, reads, writes, **kw):
    return P.add("act", lambda e: e.activation(out=out, in_=in_, func=func, **kw), reads=reads, writes=writes)


def op_tt(P, eng, out, in0, in1, op, reads, writes):
    return P.add(eng, lambda e: e.tensor_tensor(out=out, in0=in0, in1=in1, op=op), reads=reads, writes=writes)


def op_ts(P, eng, out, in0, s1, s2, op0, op1, reads, writes):
    if op1 is None:
        return P.add(eng, lambda e: e.tensor_scalar(out=out, in0=in0, scalar1=s1, scalar2=None, op0=op0), reads=reads, writes=writes)
    return P.add(eng, lambda e: e.tensor_scalar(out=out, in0=in0, scalar1=s1, scalar2=s2, op0=op0, op1=op1), reads=reads, writes=writes)


def op_stt(P, out, in0, scalar, in1, op0, op1, reads, writes):
    return P.add("dve", lambda e: e.scalar_tensor_tensor(out=out, in0=in0, scalar=scalar, in1=in1, op0=op0, op1=op1), reads=reads, writes=writes)


def op_copy(P, eng, out, in_, reads, writes):
    return P.add(eng, lambda e: e.tensor_copy(out=out, in_=in_), reads=reads, writes=writes)


def op_memset(P, eng, out, val, writes):
    return P.add(eng, lambda e: e.memset(out, val), writes=writes)


PV_NORM = 0
PV_EVEN = 72
PV_EVEN_SZ = 88
PV_COLS = PV_EVEN + 2 * PV_EVEN_SZ

WSPEC_EVEN = [("win_fm", 24 * 128 * 1024), ("win_v", 1024 * 1024), ("gates", 8 * 128 * 512), ("sgwT", 128 * 1024),
              ("wout", 8 * 128 * 2048)]
WSPEC_ODD = [("wqk", 20 * 128 * 1024), ("wv", 1024 * 256), ("wo", 8 * 128 * 1024)]
WSPEC_FFN = [("wgu", FC * 128 * 2048), ("wdn", 8 * 128 * DFF)]


def fm(v):
    return np.ascontiguousarray(v.reshape(-1, 128).T)


def lhsT_arr(w, nchunk=None):
    K, N = w.shape
    a = w.reshape(K // 128, 128, N // 128, 128).transpose(2, 1, 0, 3)
    return np.ascontiguousarray(a)


def host_prep(inp):
    shared = {}
    pv = np.zeros((128, PV_COLS), np.float32)
    for l in range(DEPTH):
        pv[:, PV_NORM + 16 * l: PV_NORM + 16 * l + 8] = fm(inp["mix_norm"][l])
        pv[:, PV_NORM + 16 * l + 8: PV_NORM + 16 * l + 16] = fm(inp["ffn_norm"][l])
    pv[:, 64:72] = fm(inp["final_norm"])
    for j in range(2):
        b = PV_EVEN + j * PV_EVEN_SZ
        cw = inp["even_conv_w"][j]
        for k in range(4):
            pv[:, b + k: b + 32: 4] = fm(cw[k])
        pv[:, b + 32: b + 40] = fm(inp["even_conv_b"][j])
        pv[:, b + 40: b + 48] = fm(inp["lru_b_r"][j, 0])
        pv[:, b + 48: b + 56] = fm(inp["lru_b_i"][j, 0])
        pv[:, b + 56: b + 64] = fm(inp["lru_b_r"][j, 1])
        pv[:, b + 64: b + 72] = fm(inp["lru_b_i"][j, 1])
        pv[:, b + 72: b + 80] = fm(inp["lru_lambda"][j, 0])
        pv[:, b + 80: b + 88] = fm(inp["lru_lambda"][j, 1])
    shared["pvec"] = pv
    for j in range(2):
        w_in = inp["even_w_in"][j]
        shared["win_fm%d" % j] = lhsT_arr(w_in[:, :3072]).reshape(-1)
        shared["win_v%d" % j] = np.ascontiguousarray(w_in[:, 3072:]).reshape(-1)
        g = np.stack([inp["lru_w_r"][j, 0], inp["lru_w_i"][j, 0], inp["lru_w_r"][j, 1], inp["lru_w_i"][j, 1]], axis=2)
        shared["gates%d" % j] = np.ascontiguousarray(g).reshape(-1)
        shared["sgwT%d" % j] = np.ascontiguousarray(inp["sg_w"][j].transpose(2, 0, 1)).reshape(-1)
        shared["wout%d" % j] = lhsT_arr(inp["even_w_out"][j]).reshape(-1)
        shared["lng%d" % j] = np.ascontiguousarray(inp["sg_ln_g"][j].reshape(1, 1024))
        shared["lnb%d" % j] = np.ascontiguousarray(inp["sg_ln_b"][j].reshape(1, 1024))
        shared["sgb%d" % j] = np.ascontiguousarray(inp["sg_b"][j].reshape(1, 1024))
        wq = inp["attn_w_qkv"][j]
        q = wq[:, :1024]
        k = wq[:, 1024:1280]

        def swap_halves(m):
            K, N = m.shape
            return np.ascontiguousarray(m.reshape(K, N // 64, 2, 32)[:, :, ::-1, :].reshape(K, N))
        qk = np.concatenate([lhsT_arr(q), lhsT_arr(swap_halves(q)), lhsT_arr(k), lhsT_arr(swap_halves(k))], axis=0)
        shared["wqk%d" % j] = qk.reshape(-1)
        shared["wv%d" % j] = np.ascontiguousarray(wq[:, 1280:]).reshape(-1)
        shared["wo%d" % j] = lhsT_arr(inp["attn_w_o"][j]).reshape(-1)
        shared["sink%d" % j] = np.ascontiguousarray(inp["attn_sinks"][j].reshape(1, 16))
    for l in range(DEPTH):
        gu = inp["ffn_w_gu"][l]
        ga = lhsT_arr(gu[:, :DFF])
        ua = lhsT_arr(gu[:, DFF:])
        shared["wgu%d" % l] = np.ascontiguousarray(np.stack([ga, ua], axis=2)).reshape(-1)
        shared["wdn%d" % l] = lhsT_arr(inp["ffn_w_down"][l]).reshape(-1)
    half = 32
    freqs = (10000.0 ** (-np.arange(half, dtype=np.float32) * 2.0 / 64)).astype(np.float32)
    ang = np.arange(S, dtype=np.float32)[None, :] * freqs[:, None]
    cos = np.cos(ang).astype(np.float32)
    sin = np.sin(ang).astype(np.float32)
    cos64 = np.concatenate([cos, cos], axis=0)
    sin64 = np.concatenate([-sin, sin], axis=0)
    shared["cosT"] = np.ascontiguousarray(np.concatenate([cos64, cos64], axis=0))
    shared["sinT"] = np.ascontiguousarray(np.concatenate([sin64, sin64], axis=0))
    tk = np.arange(128)[:, None]
    tq = np.arange(128)[None, :]
    shared["masks"] = np.ascontiguousarray(np.stack([(tk >= tq), (tk <= tq)], axis=1).astype(np.float32))
    return shared


def build(n_layers=DEPTH, final=True, stop_after=None):
    nc = bass.Bass("TRN2", target_bir_lowering=False)
    dram = {}

    def din(name, shape, dt=F32):
        dram[name] = nc.dram_tensor(name, list(shape), dt, kind="ExternalInput").ap()
        return dram[name]

    def dint(name, shape, dt):
        dram[name] = nc.dram_tensor(name, list(shape), dt, kind="Internal").ap()
        return dram[name]

    xin = din("xT", [D, S])
    outT = nc.dram_tensor("outT", [D, S], F32, kind="ExternalOutput").ap()
    din("pvec", [128, PV_COLS])
    din("cosT", [128, S])
    din("sinT", [128, S])
    din("masks", [128, 2, 128])
    wlist = []
    for l in range(DEPTH):
        j = l // 2
        specs = (WSPEC_EVEN if l % 2 == 0 else WSPEC_ODD)
        for nm, n in specs:
            wlist.append((l, "%s%d" % (nm, j), n))
        for nm, n in WSPEC_FFN:
            wlist.append((l, "%s%d" % (nm, l), n))
    for l, nm, n in wlist:
        din(nm, [n])
        dint(nm + "_b", [n], BF16)
    for j in range(2):
        din("lng%d" % j, [1, 1024])
        din("lnb%d" % j, [1, 1024])
        din("sgb%d" % j, [1, 1024])
        din("sink%d" % j, [1, 16])
    xres = dint("xres", [D, S], F32)
    XA = dint("XA", [D, S], F32)
    GA = dint("GA", [D, S], BF16)
    YA = dint("YA", [D, S], BF16)
    YB = dint("YB", [D, S], BF16)
    QT = dint("QT", [D, S], BF16)
    KT = dint("KT", [256, S], BF16)
    VV = dint("VV", [S, 256], BF16)

    st = ExitStack()
    P = Prog(nc)
    BASE = 24 * 1024
    LIMIT = 204 * 1024
    pa = SBAlloc(nc, BASE, LIMIT)
    pvec = pa.tile("pvec", [128, PV_COLS], F32)
    ones = pa.tile("ones", [128, 128], BF16)
    dum_a = pa.tile("dum_a", [128, 8], F32)
    dum_v = pa.tile("dum_v", [128, 8], F32)
    dum_p = pa.tile("dum_p", [128, 8], F32)
    negc = pa.tile("negc", [128, 2, 16], F32)
    neg2c = pa.tile("neg2c", [128, 2, 16], F32)
    sptmp = pa.tile("sptmp", [128, 16], F32)
    PH_BASE = (pa.off + 1023) // 1024 * 1024
    dummies = (dum_a.t, dum_v.t, dum_p.t)
    banks = [T(nc.alloc_psum_tensor("bank%d" % i, [128, 512], F32), "bank%d" % i) for i in range(8)]

    wbuf = {}
    op_dma(P, "sp", pvec.t[:], dram["pvec"], [], [pvec.b], pvec.b)
    op_memset(P, "pool", ones.t[:], 1.0, [ones.b])
    op_memset(P, "pool", dum_p.t[:], 0.0, [dum_p.b])
    op_memset(P, "dve", dum_v.t[:], 0.0, [dum_v.b])
    op_memset(P, "dve", dum_a.t[:], 0.0, [dum_a.b])
    cast_ops = []
    cast_q = []
    for l, nm, n in wlist:
        if l >= n_layers or stop_after in ('pro0', 'pro2'):
            continue
        b = Buf("w_" + nm)
        wbuf[nm] = b
        src = dram[nm].rearrange("(r f) -> r f", f=2048)
        dst = dram[nm + "_b"].rearrange("(r f) -> r f", f=2048)
        R = n // 2048
        r0 = 0
        while r0 < R:
            r1 = min(R, r0 + 512)
            cast_q.append((nm, dst[r0:r1, :], src[r0:r1, :], b))
            r0 = r1

    def pump(n):
        for _ in range(n):
            if not cast_q:
                return
            nm, d_, s_, b = cast_q.pop(0)
            aft = [cast_ops[-1]] if cast_ops else []
            cast_ops.append(op_dma(P, "pool", d_, s_, [], [b], b, nobar=True, after=aft))

    def ensure(names):
        while cast_q and any(c[0] in names for c in cast_q):
            pump(1)

    for j in range(0 if stop_after in ('pro0', 'pro1') else 2):
        lam = pvec.t[:, PV_EVEN + j * PV_EVEN_SZ + 72: PV_EVEN + j * PV_EVEN_SZ + 88]
        op_act(P, sptmp.t[:], lam, AF.Exp, [pvec.b], [sptmp.b], scale=-1.0)
        op_act(P, sptmp.t[:], sptmp.t[:], AF.Ln, [sptmp.b], [sptmp.b], bias=1.0)
        for d in range(2):
            op_ts(P, "dve", negc.t[:, j, d:16:2], sptmp.t[:, d * 8:(d + 1) * 8], -8.0, None, ALU.mult, None, [sptmp.b], [negc.b])
            op_ts(P, "dve", neg2c.t[:, j, d:16:2], sptmp.t[:, d * 8:(d + 1) * 8], -16.0, None, ALU.mult, None, [sptmp.b], [neg2c.b])

    def xtile_ap(dr, i):
        return dr[:, i * TT:(i + 1) * TT].rearrange("(c p) t -> p c t", p=128)

    def rmsnorm(xt, hn, sqr, lnt, rstd, ncol, bank):
        for c in range(KC):
            s = sqr[c % 2]
            op_act(P, s.t[:], xt.t[:, c, :], AF.Square, [xt.b], [s.b])
            op_mm(P, bank.t[:], ones.t[:], s.t[:], c == 0, c == KC - 1, [ones.b, s.b], [bank.b])
        op_act(P, lnt.t[:], bank.t[:], AF.Ln, [bank.b], [lnt.b], scale=1.0 / D, bias=EPS)
        op_act(P, rstd.t[:], lnt.t[:], AF.Exp, [lnt.b], [rstd.b], scale=-0.5)
        for c in range(KC):
            op_stt(P, hn.t[:, c, :], xt.t[:, c, :], pvec.t[:, ncol + c: ncol + c + 1], rstd.t[:], ALU.mult, ALU.mult,
                   [xt.b, pvec.b, rstd.b], [hn.b])

    class WRing:
        def __init__(self, slots):
            self.slots = slots
            self.i = 0

        def load(self, wname, F, c0, n):
            s = self.slots[self.i % len(self.slots)]
            self.i += 1
            src = dram[wname + "_b"].rearrange("(c p f) -> c p f", p=128, f=F)[c0:c0 + n].rearrange("n p f -> p n f")
            flat = s.t[:, 0:n * F].rearrange("p (n f) -> p n f", f=F)
            op_dma(P, "sp", flat, src, [wbuf[wname]], [s.b], s.b)
            return s

    def ffn_and_store(l, xt, al, ring, i, last):
        hn, sqr, lnt, rstd, act, sg = al["hn"], al["sqr"], al["lnt"], al["rstd"], al["act"], al["sg"]
        rmsnorm(xt, hn, sqr, lnt, rstd, PV_NORM + 16 * l + 8, banks[0])
        wn = "wgu%d" % l
        for c in range(FC):
            if c % 2 == 0:
                w = ring.load(wn, 2048, c, 2)
            wv = w.t[:, (c % 2) * 2048:(c % 2 + 1) * 2048].rearrange("p (g k j) -> p g k j", g=2, k=KC)
            bg = banks[1 + (c % 2)]
            bu = banks[3 + (c % 2)]
            for k in range(KC):
                op_mm(P, bg.t[:], wv[:, 0, k, :], hn.t[:, k, :], k == 0, k == KC - 1, [w.b, hn.b], [bg.b])
            for k in range(KC):
                op_mm(P, bu.t[:], wv[:, 1, k, :], hn.t[:, k, :], k == 0, k == KC - 1, [w.b, hn.b], [bu.b])
            s = sg[c % 2]
            op_act(P, s.t[:], bg.t[:], AF.Silu, [bg.b], [s.b])
            op_tt(P, "dve", act.t[:, c, :], bu.t[:], s.t[:], ALU.mult, [bu.b, s.b], [act.b])
        wn = "wdn%d" % l
        for oc in range(KC):
            w = ring.load(wn, DFF, oc, 1)
            bo = banks[5 + (oc % 2)]
            for k in range(FC):
                op_mm(P, bo.t[:], w.t[:, k * 128:(k + 1) * 128], act.t[:, k, :], k == 0, k == FC - 1, [w.b, act.b], [bo.b])
            op_tt(P, "dve", xt.t[:, oc, :], bo.t[:], xt.t[:, oc, :], ALU.add, [bo.b, xt.b], [xt.b])
        if last:
            for c in range(KC):
                s = sqr[c % 2]
                op_act(P, s.t[:], xt.t[:, c, :], AF.Square, [xt.b], [s.b])
                op_mm(P, banks[0].t[:], ones.t[:], s.t[:], c == 0, c == KC - 1, [ones.b, s.b], [banks[0].b])
            op_act(P, lnt.t[:], banks[0].t[:], AF.Ln, [banks[0].b], [lnt.b], scale=1.0 / D, bias=EPS)
            op_act(P, rstd.t[:], lnt.t[:], AF.Exp, [lnt.b], [rstd.b], scale=-0.5)
            for c in range(KC):
                op_stt(P, xt.t[:, c, :], xt.t[:, c, :], pvec.t[:, 64 + c: 65 + c], rstd.t[:], ALU.mult, ALU.mult,
                       [xt.b, pvec.b, rstd.b], [xt.b])
            op_dma(P, "act", xtile_ap(outT, i), xt.t[:], [xt.b], [], xt.b)
        else:
            op_dma(P, "act", xtile_ap(xres, i), xt.t[:], [xt.b], [], xt.b)

    def common_tiles(a, nslots):
        al = {}
        al["xt"] = [a.tile("xt%d" % k, [128, KC, TT], F32) for k in range(2)]
        al["hn"] = a.tile("hn", [128, KC, TT], BF16)
        al["sqr"] = [a.tile("sqr%d" % k, [128, TT], BF16) for k in range(2)]
        al["lnt"] = a.tile("lnt", [128, TT], F32)
        al["rstd"] = a.tile("rstd", [128, TT], F32)
        al["ring"] = WRing([a.tile("ws%d" % k, [128, 4096], BF16) for k in range(nslots)])
        return al

    def proj_ffn_phase(l, xsrc, wproj, nky, ysrcs):
        a = SBAlloc(nc, PH_BASE, LIMIT)
        xts = [a.tile("xt%d" % k, [128, KC, TT], F32) for k in range(3)]
        yts = [a.tile("yt%d" % k, [128, nky, TT], BF16) for k in range(2)]
        hn = a.tile("hn", [128, KC, TT], BF16)
        sqr = [a.tile("sqr%d" % k, [128, TT], BF16) for k in range(2)]
        lnt = a.tile("lnt", [128, TT], F32)
        rstd = a.tile("rstd", [128, TT], F32)
        ring = WRing([a.tile("ws%d" % k, [128, 4096], BF16) for k in range(7 if nky == 2 * KC else 9)])
        act = a.tile("act", [128, FC, TT], BF16)
        sg = [a.tile("sg%d" % k, [128, TT], F32) for k in range(2)]
        last_layer = final and l == n_layers - 1
        ncol = PV_NORM + 16 * l + 8
        Fw = nky * 128
        per = 4096 // Fw

        def load(i):
            xt, y = xts[i % 3], yts[i % 2]
            op_dma(P, "sp", xt.t[:], xtile_ap(xsrc, i), [], [xt.b], xt.b)
            fns = []
            k0 = 0
            for (ysrc, nk) in ysrcs:
                fns.append((y.t[:, k0:k0 + nk, :], xtile_ap(ysrc, i)))
                k0 += nk
            P.add("sp", lambda e: [e.dma_start(out=o_, in_=i_) for (o_, i_) in fns], writes=[y.b], owner=y.b, ndma=len(fns))

        def outproj(i):
            xt, y = xts[i % 3], yts[i % 2]
            for oc in range(KC):
                if oc % per == 0:
                    w = ring.load(wproj, Fw, oc, per)
                bk = banks[7 - (oc % 2)]
                o_ = (oc % per) * Fw
                for k in range(nky):
                    op_mm(P, bk.t[:], w.t[:, o_ + k * 128:o_ + (k + 1) * 128], y.t[:, k, :], k == 0, k == nky - 1, [w.b, y.b], [bk.b])
                op_tt(P, "dve", xt.t[:, oc, :], bk.t[:], xt.t[:, oc, :], ALU.add, [bk.b, xt.b], [xt.b])

        def norm(i, col, dst_hn):
            xt = xts[i % 3]
            for c in range(KC):
                s = sqr[c % 2]
                op_act(P, s.t[:], xt.t[:, c, :], AF.Square, [xt.b], [s.b])
                op_mm(P, banks[0].t[:], ones.t[:], s.t[:], c == 0, c == KC - 1, [ones.b, s.b], [banks[0].b])
            op_act(P, lnt.t[:], banks[0].t[:], AF.Ln, [banks[0].b], [lnt.b], scale=1.0 / D, bias=EPS)
            op_act(P, rstd.t[:], lnt.t[:], AF.Exp, [lnt.b], [rstd.b], scale=-0.5)
            for c in range(KC):
                if dst_hn:
                    op_stt(P, hn.t[:, c, :], xt.t[:, c, :], pvec.t[:, col + c: col + c + 1], rstd.t[:], ALU.mult, ALU.mult,
                           [xt.b, pvec.b, rstd.b], [hn.b])
                else:
                    op_stt(P, xt.t[:, c, :], xt.t[:, c, :], pvec.t[:, col + c: col + c + 1], rstd.t[:], ALU.mult, ALU.mult,
                           [xt.b, pvec.b, rstd.b], [xt.b])

        def gate_up(i):
            wn = "wgu%d" % l
            for c in range(FC):
                if c % 2 == 0:
                    w = ring.load(wn, 2048, c, 2)
                wv = w.t[:, (c % 2) * 2048:(c % 2 + 1) * 2048].rearrange("p (g k j) -> p g k j", g=2, k=KC)
                bg = banks[1 + (c % 2)]
                bu = banks[3 + (c % 2)]
                for k in range(KC):
                    op_mm(P, bg.t[:], wv[:, 0, k, :], hn.t[:, k, :], k == 0, k == KC - 1, [w.b, hn.b], [bg.b])
                for k in range(KC):
                    op_mm(P, bu.t[:], wv[:, 1, k, :], hn.t[:, k, :], k == 0, k == KC - 1, [w.b, hn.b], [bu.b])
                s = sg[c % 2]
                op_act(P, s.t[:], bg.t[:], AF.Silu, [bg.b], [s.b])
                op_tt(P, "dve", act.t[:, c, :], bu.t[:], s.t[:], ALU.mult, [bu.b, s.b], [act.b])

        def down(i):
            xt = xts[i % 3]
            wn = "wdn%d" % l
            for oc in range(KC):
                w = ring.load(wn, DFF, oc, 1)
                bo = banks[5 + (oc % 2)]
                for k in range(FC):
                    op_mm(P, bo.t[:], w.t[:, k * 128:(k + 1) * 128], act.t[:, k, :], k == 0, k == FC - 1, [w.b, act.b], [bo.b])
                op_tt(P, "dve", xt.t[:, oc, :], bo.t[:], xt.t[:, oc, :], ALU.add, [bo.b, xt.b], [xt.b])

        load(0)
        load(1)
        outproj(0)
        norm(0, ncol, True)
        for i in range(NT):
            xt = xts[i % 3]
            if i + 2 < NT:
                load(i + 2)
            pump(2)
            if i + 1 < NT:
                outproj(i + 1)
            gate_up(i)
            if i + 1 < NT and not last_layer:
                norm(i + 1, ncol, True)
            down(i)
            if last_layer:
                norm(i, 64, False)
                op_dma(P, "act", xtile_ap(outT, i), xt.t[:], [xt.b], [], xt.b)
                if i + 1 < NT:
                    norm(i + 1, ncol, True)
            else:
                op_dma(P, "act", xtile_ap(xres, i), xt.t[:], [xt.b], [], xt.b)
        P.barrier(dummies)

    def even_layer(l, xsrc):
        j = l // 2
        pb = PV_EVEN + j * PV_EVEN_SZ
        ensure(["win_fm%d" % j, "win_v%d" % j, "sgwT%d" % j, "gates%d" % j])
        a = SBAlloc(nc, PH_BASE, LIMIT)
        al = common_tiles(a, 3)
        ring = al["ring"]
        hn = al["hn"]
        XAs = a.tile("XAs", [128, KC, TT], F32)
        GAs = a.tile("GAs", [128, KC, TT], BF16)
        YBs = a.tile("YBs", [128, KC, TT], BF16)
        tsv = [a.tile("tsv%d" % k, [128, TT], F32) for k in range(2)]
        U = a.tile("U", [128, KC, TT], BF16)
        G = a.tile("G", [128, 4, 1024], F32)
        V = a.tile("V", [128, 4, 1024], BF16)
        lng = a.tile("lng", [128, 1024], F32)
        lnb = a.tile("lnb", [128, 1024], F32)
        sgb = a.tile("sgb", [128, 1024], F32)
        sgw = a.tile("sgw", [128, 1024], BF16)
        wv = a.tile("wv", [128, KC, 1024], BF16)
        stats = a.tile("stats", [128, 4, 2, 6], F32)
        mv = a.tile("mv", [128, 4, 2], F32)
        lrs = a.tile("lrs", [128, 4], F32)
        op_dma(P, "sp", lng.t[:], dram["lng%d" % j].partition_broadcast(128), [], [lng.b], lng.b)
        op_dma(P, "sp", lnb.t[:], dram["lnb%d" % j].partition_broadcast(128), [], [lnb.b], lnb.b)
        op_dma(P, "sp", sgb.t[:], dram["sgb%d" % j].partition_broadcast(128), [], [sgb.b], sgb.b)
        op_dma(P, "sp", sgw.t[:], dram["sgwT%d_b" % j].rearrange("(q f) -> q f", f=1024), [wbuf["sgwT%d" % j]], [sgw.b], sgw.b)
        op_dma(P, "sp", wv.t[:], dram["win_v%d_b" % j].rearrange("(k p c) -> p k c", p=128, c=1024), [wbuf["win_v%d" % j]], [wv.b], wv.b)

        def e1_tail(ti):
            for g in range(8):
                bk = banks[6 + (g % 2)]
                for sidx in range(4):
                    op_mm(P, bk.t[:, sidx * 128:(sidx + 1) * 128], V.t[:, sidx, g * 128:(g + 1) * 128], sgw.t[:, g * 128:(g + 1) * 128],
                          True, True, [V.b, sgw.b], [bk.b])
                ts_ = tsv[g % 2]
                op_tt(P, "dve", ts_.t[:].rearrange("p (s q) -> p s q", s=4), bk.t[:].rearrange("p (s q) -> p s q", s=4),
                      sgb.t[:, g * 128:(g + 1) * 128].unsqueeze(1).to_broadcast([128, 4, 128]), ALU.add, [bk.b, sgb.b], [ts_.b])
                op_tt(P, "dve", YBs.t[:, g, :], ts_.t[:], U.t[:, g, :], ALU.mult, [ts_.b, U.b], [YBs.b])

        op_dma(P, "sp", al["xt"][0].t[:], xtile_ap(xsrc, 0), [], [al["xt"][0].b], al["xt"][0].b)
        for i in range(NT):
            xt = al["xt"][i % 2]
            if i + 1 < NT:
                xn = al["xt"][(i + 1) % 2]
                op_dma(P, "sp", xn.t[:], xtile_ap(xsrc, i + 1), [], [xn.b], xn.b)
            pump(2)
            rmsnorm(xt, hn, al["sqr"], al["lnt"], al["rstd"], PV_NORM + 16 * l, banks[0])
            for oc in range(24):
                if oc % 4 == 0:
                    w = ring.load("win_fm%d" % j, 1024, oc, 4)
                bk = banks[1 + (oc % 3)]
                for k in range(KC):
                    op_mm(P, bk.t[:], w.t[:, (oc % 4) * 1024 + k * 128:(oc % 4) * 1024 + (k + 1) * 128], hn.t[:, k, :],
                          k == 0, k == KC - 1, [w.b, hn.b], [bk.b])
                c = oc % 8
                rows = slice(c * 128, (c + 1) * 128)
                cols = slice(i * TT, (i + 1) * TT)
                if oc < 8:
                    op_act(P, XAs.t[:, c, :], bk.t[:], AF.Copy, [bk.b], [XAs.b])
                    if oc == 7:
                        op_dma(P, "act", xtile_ap(XA, i), XAs.t[:], [XAs.b], [], XAs.b)
                elif oc < 16:
                    op_act(P, GAs.t[:, c, :], bk.t[:], AF.Gelu_apprx_tanh, [bk.b], [GAs.b])
                    if oc == 15:
                        op_dma(P, "act", xtile_ap(GA, i), GAs.t[:], [GAs.b], [], GAs.b)
                        if i > 0:
                            e1_tail(i - 1)
                else:
                    op_act(P, U.t[:, c, :], bk.t[:], AF.Gelu_apprx_tanh, [bk.b], [U.b])
            for sidx in range(4):
                for hh in range(2):
                    bk = banks[4 + hh]
                    for k in range(KC):
                        op_mm(P, bk.t[:], hn.t[:, k, sidx * 128:(sidx + 1) * 128], wv.t[:, k, hh * 512:(hh + 1) * 512],
                              k == 0, k == KC - 1, [hn.b, wv.b], [bk.b])
                    op_act(P, G.t[:, sidx, hh * 512:(hh + 1) * 512], bk.t[:], AF.Gelu_apprx_tanh, [bk.b], [G.b])
                    P.add("dve", (lambda e, o=stats.t[:, sidx, hh, :], i_=G.t[:, sidx, hh * 512:(hh + 1) * 512]: e.bn_stats(out=o, in_=i_)),
                          reads=[G.b], writes=[stats.b])
                P.add("dve", (lambda e, o=mv.t[:, sidx, :], i_=stats.t[:, sidx, :, :].rearrange("p a b -> p (a b)"): e.bn_aggr(out=o, in_=i_)),
                      reads=[stats.b], writes=[mv.b])
            op_act(P, lrs.t[:], mv.t[:, :, 1], AF.Ln, [mv.b], [lrs.b], bias=EPS)
            op_act(P, lrs.t[:], lrs.t[:], AF.Exp, [lrs.b], [lrs.b], scale=-0.5)
            for sidx in range(4):
                op_ts(P, "dve", G.t[:, sidx, :], G.t[:, sidx, :], mv.t[:, sidx, 0:1], lrs.t[:, sidx:sidx + 1], ALU.subtract, ALU.mult,
                      [G.b, mv.b, lrs.b], [G.b])
                op_tt(P, "pool", G.t[:, sidx, :], G.t[:, sidx, :], lng.t[:], ALU.mult, [G.b, lng.b], [G.b])
                op_tt(P, "pool", V.t[:, sidx, :], G.t[:, sidx, :], lnb.t[:], ALU.add, [G.b, lnb.b], [V.b])
            if i > 0:
                op_dma(P, "act", xtile_ap(YB, i - 1), YBs.t[:], [YBs.b], [], YBs.b)
        e1_tail(NT - 1)
        op_dma(P, "act", xtile_ap(YB, NT - 1), YBs.t[:], [YBs.b], [], YBs.b)
        P.barrier(dummies)
        if stop_after == 'E1':
            return

        a = SBAlloc(nc, PH_BASE, LIMIT)
        xa = a.tile("xa", [128, S + 4], F32)
        xc = [a.tile("xc%d" % k, [128, S], F32) for k in range(2)]
        xcb = [a.tile("xcb%d" % k, [128, S], BF16) for k in range(2)]
        rr = [a.tile("rr%d" % k, [128, TT], F32) for k in range(NT)]
        ii = [a.tile("ii%d" % k, [128, TT], F32) for k in range(NT)]
        t1 = [a.tile("t1%d" % k, [128, TT], F32) for k in range(NT)]
        hd = [a.tile("hd%d" % k, [128, S], F32) for k in range(2)]
        gat = [a.tile("gat%d" % k, [128, S], BF16) for k in range(2)]
        ya = a.tile("ya", [128, S], BF16)
        gw = [a.tile("gw%d" % k, [128, 4, 128], BF16) for k in range(2)]
        op_memset(P, "pool", xa.t[:, 0:2], 0.0, [xa.b])
        op_memset(P, "pool", xa.t[:, S + 2:S + 4], 0.0, [xa.b])
        gsrc = dram["gates%d_b" % j].rearrange("(h p f) -> h p f", p=128, f=512)

        def e2_load(h):
            rows = slice(h * 128, (h + 1) * 128)
            g_ = gw[h % 2]
            op_dma(P, "sp", g_.t[:].rearrange("p a b -> p (a b)"), gsrc[h], [wbuf["gates%d" % j]], [g_.b], g_.b)
            op_dma(P, "sp", xa.t[:, 2:S + 2], XA[rows, :], [], [xa.b], xa.b)
            op_dma(P, "sp", gat[h % 2].t[:], GA[rows, :], [], [gat[h % 2].b], gat[h % 2].b)

        def e2_conv(h):
            xc_ = xc[h % 2]
            cwc = pb + h * 4
            op_ts(P, "dve", xc_.t[:], xa.t[:, 0:S], pvec.t[:, cwc:cwc + 1], pvec.t[:, pb + 32 + h: pb + 33 + h], ALU.mult, ALU.add,
                  [xa.b, pvec.b], [xc_.b])
            for k in range(1, 4):
                op_stt(P, xc_.t[:], xa.t[:, k:k + S], pvec.t[:, cwc + k: cwc + k + 1], xc_.t[:], ALU.mult, ALU.add,
                       [xa.b, pvec.b, xc_.b], [xc_.b])
            op_act(P, xcb[h % 2].t[:], xc_.t[:], AF.Copy, [xc_.b], [xcb[h % 2].b])

        e2_load(0)
        e2_conv(0)
        for h in range(8):
            rows = slice(h * 128, (h + 1) * 128)
            g_ = gw[h % 2]
            xc_ = xc[h % 2]
            xcb_ = xcb[h % 2]
            if h + 1 < 8:
                e2_load(h + 1)
            pump(1)
            for d in range(2):
                order = list(range(NT)) if d == 0 else list(range(NT - 1, -1, -1))
                cr = pb + 40 + 16 * d + h
                for t in order:
                    cs = slice(t * TT, (t + 1) * TT)
                    br = banks[(t % 2) * 2]
                    bi = banks[(t % 2) * 2 + 1]
                    op_mm(P, br.t[:], g_.t[:, 2 * d, :], xcb_.t[:, cs], True, True, [g_.b, xcb_.b], [br.b])
                    op_mm(P, bi.t[:], g_.t[:, 2 * d + 1, :], xcb_.t[:, cs], True, True, [g_.b, xcb_.b], [bi.b])
                    op_act(P, rr[t].t[:], br.t[:], AF.Sigmoid, [br.b, pvec.b], [rr[t].b], bias=pvec.t[:, cr:cr + 1])
                    op_act(P, ii[t].t[:], bi.t[:], AF.Sigmoid, [bi.b, pvec.b], [ii[t].b], bias=pvec.t[:, cr + 8:cr + 9])
                sc = negc.t[:, j, 2 * h + d: 2 * h + d + 1]
                sc2 = neg2c.t[:, j, 2 * h + d: 2 * h + d + 1]
                for t in order:
                    op_act(P, t1[t].t[:], rr[t].t[:], AF.Exp, [rr[t].b, neg2c.b], [t1[t].b], scale=sc2)
                    op_act(P, rr[t].t[:], rr[t].t[:], AF.Exp, [rr[t].b, negc.b], [rr[t].b], scale=sc)
                    op_act(P, t1[t].t[:], t1[t].t[:], AF.Ln, [t1[t].b], [t1[t].b], scale=-1.0, bias=1.0)
                    op_act(P, t1[t].t[:], t1[t].t[:], AF.Exp, [t1[t].b], [t1[t].b], scale=0.5)
                hh_ = hd[d]
                for n_, t in enumerate(order):
                    cs = slice(t * TT, (t + 1) * TT)
                    op_tt(P, "dve", ii[t].t[:], ii[t].t[:], xc_.t[:, cs], ALU.mult, [ii[t].b, xc_.b], [ii[t].b])
                    op_tt(P, "dve", ii[t].t[:], ii[t].t[:], t1[t].t[:], ALU.mult, [ii[t].b, t1[t].b], [ii[t].b])
                    if d == 0:
                        init = 0.0 if n_ == 0 else hh_.t[:, t * TT - 1:t * TT]
                        P.add("dve", (lambda e, o=hh_.t[:, cs], a_=rr[t].t[:], u_=ii[t].t[:], i_=init: e.tensor_tensor_scan(out=o, data0=a_, data1=u_, initial=i_, op0=ALU.mult, op1=ALU.add)),
                              reads=[rr[t].b, ii[t].b, hh_.b], writes=[hh_.b])
                    else:
                        init = 0.0 if n_ == 0 else hh_.t[:, (t + 1) * TT:(t + 1) * TT + 1]
                        P.add("dve", (lambda e, o=hh_.t[:, t * TT:(t + 1) * TT][:, ::-1], a_=rr[t].t[:, ::-1], u_=ii[t].t[:, ::-1], i_=init: e.tensor_tensor_scan(out=o, data0=a_, data1=u_, initial=i_, op0=ALU.mult, op1=ALU.add)),
                              reads=[rr[t].b, ii[t].b, hh_.b], writes=[hh_.b])
                if d == 0 and h + 1 < 8:
                    e2_conv(h + 1)
            op_tt(P, "dve", hd[0].t[:], hd[0].t[:], hd[1].t[:], ALU.add, [hd[0].b, hd[1].b], [hd[0].b])
            op_tt(P, "dve", ya.t[:], hd[0].t[:], gat[h % 2].t[:], ALU.mult, [hd[0].b, gat[h % 2].b], [ya.b])
            op_dma(P, "sp", YA[rows, :], ya.t[:], [ya.b], [], ya.b)
        P.barrier(dummies)
        if stop_after == 'E2':
            return

        ensure(["wout%d" % j, "wgu%d" % l, "wdn%d" % l])
        proj_ffn_phase(l, xsrc, "wout%d" % j, 2 * KC, [(YA, KC), (YB, KC)])

    def odd_layer(l, xsrc):
        j = l // 2
        ensure(["wqk%d" % j, "wv%d" % j])
        a = SBAlloc(nc, PH_BASE, LIMIT)
        al = common_tiles(a, 8)
        ring = al["ring"]
        hn = al["hn"]
        cs_t = [a.tile("cos%d" % k, [128, TT], F32) for k in range(2)]
        sn_t = [a.tile("sin%d" % k, [128, TT], F32) for k in range(2)]
        tA = [a.tile("tA%d" % k, [128, TT], F32) for k in range(2)]
        tB = [a.tile("tB%d" % k, [128, TT], F32) for k in range(2)]
        Qs = a.tile("Qs", [128, KC, TT], BF16)
        Ks = a.tile("Ks", [128, 2, TT], BF16)
        Vs = a.tile("Vs", [128, 4, 256], BF16)
        wv = a.tile("wv", [128, KC, 256], BF16)
        op_dma(P, "sp", wv.t[:], dram["wv%d_b" % j].rearrange("(k p c) -> p k c", p=128, c=256), [wbuf["wv%d" % j]], [wv.b], wv.b)

        def load_tile(i):
            xt = al["xt"][i % 2]
            op_dma(P, "sp", xt.t[:], xtile_ap(xsrc, i), [], [xt.b], xt.b)
            op_dma(P, "sp", cs_t[i % 2].t[:], dram["cosT"][:, i * TT:(i + 1) * TT], [], [cs_t[i % 2].b], cs_t[i % 2].b)
            op_dma(P, "sp", sn_t[i % 2].t[:], dram["sinT"][:, i * TT:(i + 1) * TT], [], [sn_t[i % 2].b], sn_t[i % 2].b)
        load_tile(0)
        cnt = 0
        for i in range(NT):
            xt = al["xt"][i % 2]
            cs_, sn_ = cs_t[i % 2], sn_t[i % 2]
            if i + 1 < NT:
                load_tile(i + 1)
            pump(2)
            rmsnorm(xt, hn, al["sqr"], al["lnt"], al["rstd"], PV_NORM + 16 * l, banks[0])
            for grp, (c0, nchunk, dst) in enumerate([(0, 8, QT), (16, 2, KT)]):
                for cc in range(nchunk):
                    if cc % 2 == 0:
                        n_ = min(2, nchunk - cc)
                        wN = ring.load("wqk%d" % j, 1024, c0 + cc, n_)
                        wS = ring.load("wqk%d" % j, 1024, c0 + nchunk + cc, n_)
                    bn_ = banks[1 + (cnt % 2) * 2]
                    bs_ = banks[2 + (cnt % 2) * 2]
                    o_ = (cc % 2) * 1024
                    for k in range(KC):
                        op_mm(P, bn_.t[:], wN.t[:, o_ + k * 128:o_ + (k + 1) * 128], hn.t[:, k, :], k == 0, k == KC - 1, [wN.b, hn.b], [bn_.b])
                    for k in range(KC):
                        op_mm(P, bs_.t[:], wS.t[:, o_ + k * 128:o_ + (k + 1) * 128], hn.t[:, k, :], k == 0, k == KC - 1, [wS.b, hn.b], [bs_.b])
                    ta, tb = tA[cnt % 2], tB[cnt % 2]
                    stg_ = Qs if grp == 0 else Ks
                    op_tt(P, "dve", ta.t[:], bn_.t[:], cs_.t[:], ALU.mult, [bn_.b, cs_.b], [ta.b])
                    op_tt(P, "dve", tb.t[:], bs_.t[:], sn_.t[:], ALU.mult, [bs_.b, sn_.b], [tb.b])
                    op_tt(P, "pool", stg_.t[:, cc, :], ta.t[:], tb.t[:], ALU.add, [ta.b, tb.b], [stg_.b])
                    cnt += 1
                if grp == 0:
                    op_dma(P, "act", xtile_ap(QT, i), Qs.t[:], [Qs.b], [], Qs.b)
                else:
                    op_dma(P, "act", xtile_ap(KT, i), Ks.t[:], [Ks.b], [], Ks.b)
            for sidx in range(4):
                bk = banks[5 + (sidx % 2)]
                for k in range(KC):
                    op_mm(P, bk.t[:, 0:256], hn.t[:, k, sidx * 128:(sidx + 1) * 128], wv.t[:, k, :], k == 0, k == KC - 1, [hn.b, wv.b], [bk.b])
                op_act(P, Vs.t[:, sidx, :], bk.t[:, 0:256], AF.Copy, [bk.b], [Vs.b])
            op_dma(P, "act", VV[i * TT:(i + 1) * TT, :].rearrange("(s p) c -> p s c", p=128), Vs.t[:], [Vs.b], [], Vs.b)
        P.barrier(dummies)
        if stop_after == 'O1':
            return

        a = SBAlloc(nc, PH_BASE, LIMIT)
        NB = S // 128
        kt = [a.tile("kt%d" % k, [128, S], BF16) for k in range(2)]
        qt = [a.tile("qt%d" % k, [128, 4, S], BF16) for k in range(2)]
        vt = [a.tile("vt%d" % k, [128, NB, 128], BF16) for k in range(2)]
        ot = a.tile("ot", [64, 4, S], BF16)
        NE = 8
        et = [a.tile("et%d" % k, [128, 4, 128], BF16) for k in range(NE)]
        dt_ = [a.tile("dt%d" % k, [64, 4, 128], F32) for k in range(4)]
        mk32 = a.tile("mk32", [128, 2, 128], F32)
        mk = a.tile("mk", [128, 2, 128], BF16)
        snk = a.tile("snk", [128, 16], F32)
        esb = a.tile("esb", [128, 16, 128], F32)
        for k in range(2):
            op_memset(P, "pool", kt[k].t[64:128, :], 0.0, [kt[k].b])
            op_memset(P, "pool", qt[k].t[64:128, :, :], 0.0, [qt[k].b])
            op_memset(P, "pool", vt[k].t[:, :, 64:128], 1.0, [vt[k].b])
        op_dma(P, "sp", mk32.t[:], dram["masks"], [], [mk32.b], mk32.b)
        op_copy(P, "dve", mk.t[:], mk32.t[:], [mk32.b], [mk.b])
        op_dma(P, "sp", snk.t[:], dram["sink%d" % j].partition_broadcast(128), [], [snk.b], snk.b)
        op_act(P, snk.t[:], snk.t[:], AF.Exp, [snk.b], [snk.b])
        op_copy(P, "dve", esb.t[:], snk.t[:].unsqueeze(2).to_broadcast([128, 16, 128]), [snk.b], [esb.b])

        def load_g(g):
            k_, q_, v_ = kt[g % 2], qt[g % 2], vt[g % 2]
            op_dma(P, "sp", k_.t[0:64, :], KT[g * 64:(g + 1) * 64, :], [], [k_.b], k_.b)
            op_dma(P, "sp", q_.t[0:64, :, :], QT[g * 256:(g + 1) * 256, :].rearrange("(h d) t -> d h t", d=64), [], [q_.b], q_.b)
            op_dma(P, "sp", v_.t[:, :, 0:64], VV[:, g * 64:(g + 1) * 64].rearrange("(n p) d -> p n d", p=128), [], [v_.b], v_.b)
        load_g(0)
        ctr = {"ec": 0}

        def scores(g, ib):
            k_, q_ = kt[g % 2], qt[g % 2]
            js = [jb for jb in (ib - 1, ib, ib + 1) if 0 <= jb < NB]
            qs = q_.t[:, :, ib * 128:(ib + 1) * 128]
            es = []
            for jb in js:
                ec = ctr["ec"]
                ctr["ec"] += 1
                bs_ = banks[ec % 4]
                e_ = et[ec % NE]
                op_mm(P, bs_.t[:].rearrange("p (h q) -> p h q", h=4), k_.t[:, jb * 128:(jb + 1) * 128], qs, True, True, [k_.b, q_.b], [bs_.b])
                op_act(P, e_.t[:].rearrange("p h q -> p (h q)"), bs_.t[:], AF.Exp, [bs_.b], [e_.b], scale=0.125)
                if jb != ib:
                    mi = 1 if jb > ib else 0
                    op_tt(P, "dve", e_.t[:], e_.t[:], mk.t[:, mi, :].unsqueeze(1).to_broadcast([128, 4, 128]), ALU.mult, [e_.b, mk.b], [e_.b])
                es.append((jb, e_))
            return es

        def pv(g, ib, es):
            v_ = vt[g % 2]
            bo = banks[4 + (ib % 4)]
            for n_, (jb, e_) in enumerate(es):
                ef = e_.t[:].rearrange("p h q -> p (h q)")
                op_mm(P, bo.t[:], v_.t[:, jb, :], ef, n_ == 0, n_ == len(es) - 1, [v_.b, e_.b], [bo.b])
            d_ = dt_[ib % 4]
            op_tt(P, "dve", d_.t[:], bo.t[64:128, :].rearrange("p (h q) -> p h q", h=4), esb.t[64:128, g * 4:(g + 1) * 4, :], ALU.add, [bo.b, esb.b], [d_.b])

        def nrm(g, ib):
            bo = banks[4 + (ib % 4)]
            d_ = dt_[ib % 4]
            df = d_.t[:].rearrange("p h q -> p (h q)")
            op_act(P, df, df, AF.Ln, [d_.b], [d_.b])
            op_act(P, df, df, AF.Exp, [d_.b], [d_.b], scale=-1.0)
            op_tt(P, "dve", ot.t[:, :, ib * 128:(ib + 1) * 128], bo.t[0:64, :].rearrange("p (h q) -> p h q", h=4), d_.t[:], ALU.mult, [bo.b, d_.b], [ot.b])

        for g in range(4):
            if g + 1 < 4:
                load_g(g + 1)
            pump(2)
            prev = None
            for ib in range(NB):
                es = scores(g, ib)
                if prev is not None:
                    pv(g, prev[0], prev[1])
                if ib >= 2:
                    nrm(g, ib - 2)
                prev = (ib, es)
            pv(g, prev[0], prev[1])
            nrm(g, NB - 2)
            nrm(g, NB - 1)
            op_dma(P, "sp", YA[g * 256:(g + 1) * 256, :].rearrange("(h d) t -> d h t", d=64), ot.t[:], [ot.b], [], ot.b)
        P.barrier(dummies)
        if stop_after == 'O2':
            return

        ensure(["wo%d" % j, "wgu%d" % l, "wdn%d" % l])
        proj_ffn_phase(l, xsrc, "wo%d" % j, KC, [(YA, KC)])

    for l in range(n_layers if stop_after not in ('pro', 'pro0', 'pro1', 'pro2') else 0):
        xsrc = xin if l == 0 else xres
        if l % 2 == 0:
            even_layer(l, xsrc)
        else:
            odd_layer(l, xsrc)
    if not final:
        a = SBAlloc(nc, PH_BASE, LIMIT)
        xt = a.tile("xo", [128, KC, TT], F32)
        for i in range(NT):
            op_dma(P, "sp", xt.t[:], xtile_ap(xres, i), [], [xt.b], xt.b)
            op_dma(P, "sp", xtile_ap(outT, i), xt.t[:], [xt.b], [], xt.b)
    P.barrier(dummies)
    P.add("sp", lambda e: e.nop())
    P.emit(st)
    st.close()
    return nc, P


def kernel(**inputs):
    inp = {k: np.asarray(v) for k, v in inputs.items()}
    shared = host_prep(inp)
    nc, _ = build()
    x = inp["x"]
    in_maps = []
    for c in range(NCORES):
        m = dict(shared)
        m["xT"] = np.ascontiguousarray(x[c].T)
        in_maps.append(m)
    res = run_bass_kernel_spmd(nc, in_maps, core_ids=list(range(NCORES)))
    out = np.stack([np.asarray(res.results[c]["outT"]).T for c in range(NCORES)], axis=0)
    return np.ascontiguousarray(out.astype(np.float32))
```
